# Optimizing a Trainium2 kernel written in Bass

```python
import functools
import jax, jax.numpy as jnp
from jax import lax
import numpy as np

D_MODEL = 1024
BATCH = 4
SEQ = 4096
DEPTH = 1
DEC_BATCH = 32
DEC_SEQ = 1
PAST_LEN = 16384
PAGE_SIZE = 128

N_HEADS = 8
HEAD_DIM = 64
ATTN_WIDTH = N_HEADS * HEAD_DIM
N_GROUPS = 8
GROUP_DIM = 64
GMLP_WIDTH = N_GROUPS * GROUP_DIM
CHUNK = 128
Q_BLOCK = 128
D_FF = 2816
PLE_DIM = 256
FORGET_BIAS_MIN = 4.0
FORGET_BIAS_MAX = 10.0
EPS = 1e-6
NEG_INF = -1e30
SPLIT_SIZES = (ATTN_WIDTH, ATTN_WIDTH, ATTN_WIDTH, N_HEADS, GMLP_WIDTH, GMLP_WIDTH, D_MODEL, D_MODEL)
D_IN = sum(SPLIT_SIZES)
SPLIT_POINTS = tuple(int(c) for c in np.cumsum(SPLIT_SIZES)[:-1])

kernel_name = 'fox_gmlp_macaron_hybrid_step'


def rms_norm(x, g):
    xf = x.astype(jnp.float32)
    y = xf * lax.rsqrt(jnp.mean(xf * xf, axis=-1, keepdims=True) + EPS)
    return (y * g.astype(jnp.float32)).astype(x.dtype)


def swiglu_ffn(x, g, w_gu, w_down):
    a, b = jnp.split(rms_norm(x, g) @ w_gu, 2, axis=-1)
    return (jax.nn.silu(a) * b) @ w_down


def mixer_inputs(h, w_in, b_forget, q_norm, k_norm, gmlp_v_norm):
    sh = h.shape[:-1]
    q, k, v, f, u, gv, ga, gb = jnp.split(h @ w_in, SPLIT_POINTS, axis=-1)
    q = rms_norm(q.reshape(sh + (N_HEADS, HEAD_DIM)), q_norm)
    k = rms_norm(k.reshape(sh + (N_HEADS, HEAD_DIM)), k_norm)
    v = v.reshape(sh + (N_HEADS, HEAD_DIM))
    logf = jax.nn.log_sigmoid((f + b_forget).astype(jnp.float32))
    u = jax.nn.gelu(u)
    gv = rms_norm(jax.nn.gelu(gv).reshape(sh + (N_GROUPS, GROUP_DIM)), gmlp_v_norm)
    return q, k, v, logf, u, gv, jax.nn.sigmoid(ga), jax.nn.sigmoid(gb)


def fox_attention_prompt(q, k, v, logf):
    b, s = q.shape[:2]
    scale = HEAD_DIM ** -0.5
    f_t = jnp.cumsum(logf, axis=1).transpose(0, 2, 1)
    kpos = jnp.arange(s)

    def block(i):
        start = i * Q_BLOCK
        qb = lax.dynamic_slice_in_dim(q, start, Q_BLOCK, axis=1)
        fb = lax.dynamic_slice_in_dim(f_t, start, Q_BLOCK, axis=2)
        sc = jnp.einsum('bthd,bshd->bhts', qb, k, preferred_element_type=jnp.float32) * scale
        sc = sc + fb[:, :, :, None] - f_t[:, :, None, :]
        qpos = start + jnp.arange(Q_BLOCK)
        sc = jnp.where(kpos[None, :] <= qpos[:, None], sc, NEG_INF)
        p = jax.nn.softmax(sc, axis=-1)
        return jnp.einsum('bhts,bshd->bthd', p.astype(v.dtype), v)

    out = lax.map(block, jnp.arange(s // Q_BLOCK))
    return out.transpose(1, 0, 2, 3, 4).reshape(b, s, ATTN_WIDTH)


def fox_attention_sample(q, k, v, logf, cache_k, cache_v, cache_logf, page_table):
    db, t = q.shape[:2]
    scale = HEAD_DIM ** -0.5
    k_past = cache_k[page_table]
    v_past = cache_v[page_table]
    lf_past = cache_logf[page_table].astype(jnp.float32)
    n_pages, page = lf_past.shape[1], lf_past.shape[2]
    n_past = n_pages * page
    lf_past = lf_past.reshape(db, n_past, N_HEADS)
    suffix = lax.cumsum(lf_past, axis=1, reverse=True) - lf_past
    f_new_t = jnp.cumsum(logf, axis=1).transpose(0, 2, 1)
    s_past = jnp.einsum('bthd,bnphd->bhtnp', q, k_past, preferred_element_type=jnp.float32)
    s_past = s_past.reshape(db, N_HEADS, t, n_past) * scale
    s_past = s_past + suffix.transpose(0, 2, 1)[:, :, None, :] + f_new_t[:, :, :, None]
    s_new = jnp.einsum('bthd,bshd->bhts', q, k, preferred_element_type=jnp.float32) * scale
    s_new = s_new + f_new_t[:, :, :, None] - f_new_t[:, :, None, :]
    s_new = jnp.where(jnp.tril(jnp.ones((t, t), dtype=bool)), s_new, NEG_INF)
    probs = jax.nn.softmax(jnp.concatenate([s_past, s_new], axis=-1), axis=-1)
    p_past = probs[..., :n_past].reshape(db, N_HEADS, t, n_pages, page).astype(v.dtype)
    p_new = probs[..., n_past:].astype(v.dtype)
    out = (jnp.einsum('bhtnp,bnphd->bthd', p_past, v_past)
           + jnp.einsum('bhts,bshd->bthd', p_new, v))
    return out.reshape(db, t, ATTN_WIDTH)


def gmlp_prompt(u, gv, w_s, b_s):
    b, s = gv.shape[:2]
    ws = jnp.where(jnp.tril(jnp.ones((CHUNK, CHUNK), dtype=bool)), w_s, 0.0)
    vc = gv.reshape(b, s // CHUNK, CHUNK, N_GROUPS, GROUP_DIM)
    sp = jnp.einsum('gts,bcsgd->bctgd', ws, vc) + b_s.T[None, None, :, :, None]
    return u * sp.reshape(b, s, GMLP_WIDTH)


def gmlp_sample(u, gv, w_s, b_s):
    db, t = gv.shape[:2]
    ws = jnp.where(jnp.tril(jnp.ones((t, t), dtype=bool)), w_s[:, :t, :t], 0.0)
    sp = jnp.einsum('gts,bsgd->btgd', ws, gv) + b_s[:, :t].T[None, :, :, None]
    return u * sp.reshape(db, t, GMLP_WIDTH)


def per_layer_embedding(x, p_emb, g, w_gate, w_proj):
    return x + jax.nn.sigmoid(rms_norm(x, g) @ w_gate) * (p_emb @ w_proj)


def decoder_layer(x, p_emb, attend, spatial_mix, ffn1_norm, ffn1_w_gu, ffn1_w_down, mix_norm, w_in,
                  b_forget, q_norm, k_norm, gmlp_v_norm, w_proj_attn, w_proj_gmlp, w_out,
                  ffn2_norm, ffn2_w_gu, ffn2_w_down, ple_norm, ple_w_gate, ple_w_proj):
    x = x + 0.5 * swiglu_ffn(x, ffn1_norm, ffn1_w_gu, ffn1_w_down)
    h = rms_norm(x, mix_norm)
    q, k, v, logf, u, gv, gate_a, gate_b = mixer_inputs(h, w_in, b_forget, q_norm, k_norm, gmlp_v_norm)
    a = attend(q, k, v, logf)
    m = spatial_mix(u, gv)
    merged = gate_a * (a @ w_proj_attn) + gate_b * (m @ w_proj_gmlp)
    x = x + merged @ w_out
    x = x + 0.5 * swiglu_ffn(x, ffn2_norm, ffn2_w_gu, ffn2_w_down)
    x = per_layer_embedding(x, p_emb, ple_norm, ple_w_gate, ple_w_proj)
    return x, k, v, logf, gv


def setup_inputs(seed: int = 0) -> dict:
    key = jax.random.key(seed)
    ks = iter(jax.random.split(key, 40))

    def nrm(shape, scale):
        return scale * jax.random.normal(next(ks), shape, jnp.float32)

    def gain(shape):
        return 1.0 + nrm(shape, 0.1)

    n_pages = PAST_LEN // PAGE_SIZE
    n_used = DEC_BATCH * n_pages
    n_pool = n_used + (n_used + 3) // 4
    perm = jax.random.permutation(next(ks), n_pool)
    page_table = perm[:n_used].reshape(DEC_BATCH, n_pages).astype(jnp.int32)
    dscale = D_MODEL ** -0.5
    b_forget = (jnp.linspace(FORGET_BIAS_MIN, FORGET_BIAS_MAX, N_HEADS, dtype=jnp.float32)[None, :]
                + nrm((DEPTH, N_HEADS), 0.1))
    cache_logf = jax.nn.log_sigmoid(b_forget[:, None, None, :] + nrm((DEPTH, n_pool, PAGE_SIZE, N_HEADS), 1.0))
    return {
        'x_prompt': nrm((BATCH, SEQ, D_MODEL), 1.0),
        'x_sample': nrm((DEC_BATCH, DEC_SEQ, D_MODEL), 1.0),
        'cache_k': nrm((DEPTH, n_pool, PAGE_SIZE, N_HEADS, HEAD_DIM), 1.0),
        'cache_v': nrm((DEPTH, n_pool, PAGE_SIZE, N_HEADS, HEAD_DIM), 1.0),
        'cache_logf': cache_logf,
        'page_table': page_table,
        'p_prompt': nrm((DEPTH, BATCH, SEQ, PLE_DIM), 1.0),
        'p_sample': nrm((DEPTH, DEC_BATCH, DEC_SEQ, PLE_DIM), 1.0),
        'ffn1_norm': gain((DEPTH, D_MODEL)),
        'ffn1_w_gu': nrm((DEPTH, D_MODEL, 2 * D_FF), dscale),
        'ffn1_w_down': nrm((DEPTH, D_FF, D_MODEL), D_FF ** -0.5),
        'mix_norm': gain((DEPTH, D_MODEL)),
        'w_in': nrm((DEPTH, D_MODEL, D_IN), dscale),
        'b_forget': b_forget,
        'q_norm': gain((DEPTH, HEAD_DIM)),
        'k_norm': gain((DEPTH, HEAD_DIM)),
        'gmlp_v_norm': gain((DEPTH, GROUP_DIM)),
        'w_spatial': nrm((DEPTH, N_GROUPS, CHUNK, CHUNK), CHUNK ** -0.5),
        'b_spatial': gain((DEPTH, N_GROUPS, CHUNK)),
        'w_proj_attn': nrm((DEPTH, ATTN_WIDTH, D_MODEL), ATTN_WIDTH ** -0.5),
        'w_proj_gmlp': nrm((DEPTH, GMLP_WIDTH, D_MODEL), GMLP_WIDTH ** -0.5),
        'w_out': nrm((DEPTH, D_MODEL, D_MODEL), dscale),
        'ffn2_norm': gain((DEPTH, D_MODEL)),
        'ffn2_w_gu': nrm((DEPTH, D_MODEL, 2 * D_FF), dscale),
        'ffn2_w_down': nrm((DEPTH, D_FF, D_MODEL), D_FF ** -0.5),
        'ple_norm': gain((DEPTH, D_MODEL)),
        'ple_w_gate': nrm((DEPTH, D_MODEL, D_MODEL), dscale),
        'ple_w_proj': nrm((DEPTH, PLE_DIM, D_MODEL), PLE_DIM ** -0.5),
    }


def reference(x_prompt, x_sample, cache_k, cache_v, cache_logf, page_table, p_prompt, p_sample,
              ffn1_norm, ffn1_w_gu, ffn1_w_down, mix_norm, w_in, b_forget, q_norm, k_norm,
              gmlp_v_norm, w_spatial, b_spatial, w_proj_attn, w_proj_gmlp, w_out,
              ffn2_norm, ffn2_w_gu, ffn2_w_down, ple_norm, ple_w_gate, ple_w_proj):
    xp, xs = x_prompt, x_sample
    kp_l, vp_l, fp_l, ks_l, vs_l, fs_l, gs_l = [], [], [], [], [], [], []
    for i in range(DEPTH):
        lw = (ffn1_norm[i], ffn1_w_gu[i], ffn1_w_down[i], mix_norm[i], w_in[i], b_forget[i],
              q_norm[i], k_norm[i], gmlp_v_norm[i], w_proj_attn[i], w_proj_gmlp[i], w_out[i],
              ffn2_norm[i], ffn2_w_gu[i], ffn2_w_down[i], ple_norm[i], ple_w_gate[i], ple_w_proj[i])
        mix_p = functools.partial(gmlp_prompt, w_s=w_spatial[i], b_s=b_spatial[i])
        mix_s = functools.partial(gmlp_sample, w_s=w_spatial[i], b_s=b_spatial[i])
        att_s = functools.partial(fox_attention_sample, cache_k=cache_k[i], cache_v=cache_v[i],
                                  cache_logf=cache_logf[i], page_table=page_table)
        xp, kp, vp, fp, _ = decoder_layer(xp, p_prompt[i], fox_attention_prompt, mix_p, *lw)
        xs, k_s, v_s, f_s, g_s = decoder_layer(xs, p_sample[i], att_s, mix_s, *lw)
        kp_l.append(kp); vp_l.append(vp); fp_l.append(fp)
        ks_l.append(k_s); vs_l.append(v_s); fs_l.append(f_s); gs_l.append(g_s)
    k_prompt = jnp.stack(kp_l)
    v_prompt = jnp.stack(vp_l)
    logf_prompt = jnp.stack(fp_l)
    k_sample = jnp.stack(ks_l)
    v_sample = jnp.stack(vs_l)
    logf_sample = jnp.stack(fs_l)
    gmlp_v_sample = jnp.stack(gs_l)
    return (xp, xs, k_prompt, v_prompt, logf_prompt, k_sample, v_sample, logf_sample, gmlp_v_sample)
```

```python
import numpy as np
from contextlib import ExitStack
import ml_dtypes
import concourse.bass as bass
import concourse.mybir as mybir
from concourse.bass_utils import run_bass_kernel_spmd

F32 = mybir.dt.float32
BF16 = mybir.dt.bfloat16
I32 = mybir.dt.int32
ALU = mybir.AluOpType
AF = mybir.ActivationFunctionType
AX = mybir.AxisListType

D_MODEL = 1024
KC = 8
NT = 2048
NS = 32
NTOK = NT + NS
TW = 416
NTT = 5
D_FF = 2816
NFC = 22
FF_GROUPS = [(0, 8), (8, 8), (16, 6)]
N_POOL = 5120
EPS = 1e-6
GELU_C = 1.5957691216057308
NEG = -30000.0

SAME_ENG_SYNC = True


class T:
    __slots__ = ("name", "last_w", "readers")

    def __init__(self, name=""):
        self.name = name
        self.last_w = None
        self.readers = []


class Chan:
    def __init__(self, sem):
        self.sem = sem
        self.count = 0


class Op:
    __slots__ = ("eng", "fn", "waits", "needs_sig", "is_dma", "chan", "sig_val", "inc")

    def __init__(self, eng, fn, is_dma, chan, inc):
        self.eng = eng
        self.fn = fn
        self.waits = []
        self.needs_sig = False
        self.is_dma = is_dma
        self.chan = chan
        self.sig_val = None
        self.inc = inc


class Prog:
    ENGS = ("pe", "act", "dve", "pool", "sp")

    def __init__(self, nc, stack):
        self.nc = nc
        self.stack = stack
        self.ops = {e: [] for e in self.ENGS}
        self.csem = {e: stack.enter_context(nc.semaphore("c_" + e)) for e in self.ENGS}
        self.nchan = 0

    def chan(self, name=None):
        self.nchan += 1
        s = self.stack.enter_context(self.nc.semaphore(name or f"d{self.nchan}"))
        return Chan(s)

    def add(self, eng, fn, reads=(), writes=(), chan=None, inc=16, wr_nosame=False):
        is_dma = chan is not None
        op = Op(eng, fn, is_dma, chan, inc)
        deps = {}
        for t in reads:
            if t.last_w is not None:
                deps[id(t.last_w)] = (t.last_w, "raw")
        for t in writes:
            if t.last_w is not None and id(t.last_w) not in deps:
                deps[id(t.last_w)] = (t.last_w, "waw")
            for r in t.readers:
                if id(r) not in deps:
                    deps[id(r)] = (r, "war")
        for d, kind in deps.values():
            if d.eng == eng and kind == "waw" and wr_nosame and d.is_dma == is_dma:
                continue
            if d.eng == eng and not d.is_dma and not is_dma:
                if eng == "pe" or kind == "war" or not SAME_ENG_SYNC:
                    continue
            d.needs_sig = True
            op.waits.append(d)
        for t in reads:
            t.readers.append(op)
        for t in writes:
            t.last_w = op
            t.readers = []
        if is_dma:
            chan.count += inc
            op.sig_val = chan.count
            op.needs_sig = True
        self.ops[eng].append(op)
        return op

    def _emit_engine(self, e, engobj):
        waited = {}
        for op in self.ops[e]:
            for d in op.waits:
                if d.is_dma:
                    sem, val = d.chan.sem, d.sig_val
                else:
                    sem, val = self.csem[d.eng], d.sig_val
                k = id(sem)
                if waited.get(k, 0) >= val:
                    continue
                waited[k] = val
                engobj.wait_ge(sem, val)
            inst = op.fn(engobj)
            if op.needs_sig:
                if op.is_dma:
                    inst.then_inc(op.chan.sem, op.inc)
                else:
                    inst.then_inc(self.csem[e], 1)

    def run(self, final_dma_ops=()):
        for e in self.ENGS:
            c = 0
            for op in self.ops[e]:
                if not op.is_dma and op.needs_sig:
                    c += 1
                    op.sig_val = c
        best = {}
        for d in final_dma_ops:
            k = id(d.chan.sem)
            if k not in best or best[k][1] < d.sig_val:
                best[k] = (d.chan.sem, d.sig_val)
        nc = self.nc
        with nc.Block() as block:
            @block.tensor
            def _(eng):
                self._emit_engine("pe", eng)

            @block.scalar
            def _(eng):
                self._emit_engine("act", eng)

            @block.vector
            def _(eng):
                self._emit_engine("dve", eng)

            @block.gpsimd
            def _(eng):
                self._emit_engine("pool", eng)

            @block.sync
            def _(eng):
                self._emit_engine("sp", eng)
                for sem, val in best.values():
                    eng.wait_ge(sem, val)


class Slots:
    def __init__(self, P, alloc, name, n, shape, dt, with_chan=True, tiles=None):
        self.tiles = tiles if tiles is not None else [alloc(f"{name}{i}", shape, dt) for i in range(n)]
        self.toks = [T(f"{name}{i}") for i in range(n)]
        self.chans = [P.chan() for i in range(n)] if with_chan else [None] * n
        self.n = n
        self.i = -1

    def next(self):
        self.i = (self.i + 1) % self.n
        return self.tiles[self.i], self.toks[self.i], self.chans[self.i]


class WStream:
    def __init__(self, P, alloc, name, shape, nf=2, nb=3, depth=None):
        self.P = P
        self.f = Slots(P, alloc, name + "f", nf, shape, F32)
        self.b = Slots(P, alloc, name + "b", nb, shape, BF16, with_chan=False)
        self.plan = []
        self.issued = 0
        self.depth = (nb - 1) if depth is None else depth

    def extend(self, items):
        self.plan.extend(items)

    def _issue(self, i):
        P = self.P
        parts = self.plan[i]
        ft, ftok, fch = self.f.next()
        for dsts, src in parts:
            P.add("sp", (lambda e, dsts=dsts, src=src, ft=ft: e.dma_start(out=dsts(ft), in_=src)), writes=[ftok], chan=fch, wr_nosame=True)
        bt, btok, _ = self.b.next()
        for dsts, src in parts:
            P.add("pool", (lambda e, dsts=dsts, ft=ft, bt=bt: e.tensor_copy(out=dsts(bt), in_=dsts(ft))), reads=[ftok], writes=[btok], wr_nosame=True)
        return bt, btok

    def get(self, i):
        while self.issued <= min(i + self.depth, len(self.plan) - 1):
            self.ready = getattr(self, "ready", {})
            self.ready[self.issued] = self._issue(self.issued)
            self.issued += 1
        return self.ready.pop(i)


def emit_rmsnorm_fm(P, C, XT, t_xt, HT, t_ht, gT, t_g, psum_tiles, name):
    sq, t_sq = C["sq"], C["t_sq"]
    rstd, t_rstd = C["rstd"], C["t_rstd"]
    for t in range(NTT):
        sl = slice(t * TW, (t + 1) * TW)
        ps, t_ps = psum_tiles[t % len(psum_tiles)]
        for kc in range(KC):
            P.add("act", (lambda e, kc=kc, sl=sl: e.activation(out=sq[:, kc, :], in_=XT[:, kc, sl], func=AF.Square)),
                  reads=[t_xt[kc]], writes=[t_sq[kc]])
        for kc in range(KC):
            P.add("pe", (lambda e, kc=kc, ps=ps: e.matmul(ps[:, 0:TW], lhsT=C["ones_bf"][:], rhs=sq[:, kc, :], start=(kc == 0), stop=(kc == KC - 1))),
                  reads=[t_sq[kc], C["t_ones"]], writes=[t_ps])
        P.add("act", (lambda e, ps=ps: e.activation(out=rstd[:], in_=ps[:, 0:TW], func=AF.Sqrt, bias=C["eps_col"][:], scale=1.0 / D_MODEL)),
              reads=[t_ps, C["t_eps"]], writes=[t_rstd])
        P.add("dve", (lambda e: e.reciprocal(out=rstd[:], in_=rstd[:])), reads=[t_rstd], writes=[t_rstd])
        for kc in range(KC):
            P.add("dve", (lambda e, kc=kc, sl=sl: e.scalar_tensor_tensor(out=HT[:, kc, sl], in0=XT[:, kc, sl], scalar=gT[:, kc:kc + 1],
                                                                         in1=rstd[:], op0=ALU.mult, op1=ALU.mult)),
                  reads=[t_xt[kc], t_g, t_rstd], writes=[t_ht[kc]], wr_nosame=True)


def emit_ffn(P, C, ws, wbase, XT, t_xt, HT, t_ht, HID, t_hid, w_gu, w_down, banks):
    sa, t_sa = C["sa"], C["t_sa"]
    wi = wbase
    for (c0, ng) in FF_GROUPS:
        for j in range(ng):
            wt, wtok = ws.get(wi); wi += 1
            for t in range(NTT):
                sl = slice(t * TW, (t + 1) * TW)
                par = (j * NTT + t) % 2
                pa, t_pa = banks[2 * par]
                pb, t_pb = banks[2 * par + 1]
                for kc in range(KC):
                    P.add("pe", (lambda e, kc=kc, sl=sl, pa=pa, wt=wt: e.matmul(pa[:, 0:TW], lhsT=wt[:, kc, 0:128], rhs=HT[:, kc, sl],
                                                                               start=(kc == 0), stop=(kc == KC - 1))),
                          reads=[wtok, t_ht[kc]], writes=[t_pa])
                for kc in range(KC):
                    P.add("pe", (lambda e, kc=kc, sl=sl, pb=pb, wt=wt: e.matmul(pb[:, 0:TW], lhsT=wt[:, kc, 128:256], rhs=HT[:, kc, sl],
                                                                               start=(kc == 0), stop=(kc == KC - 1))),
                          reads=[wtok, t_ht[kc]], writes=[t_pb])
                s_ap, s_tok = sa[par], t_sa[par]
                P.add("act", (lambda e, pa=pa, s_ap=s_ap: e.activation(out=s_ap[:], in_=pa[:, 0:TW], func=AF.Silu)),
                      reads=[t_pa], writes=[s_tok])
                P.add("dve", (lambda e, pb=pb, s_ap=s_ap, j=j, sl=sl: e.tensor_tensor(out=HID[:, j, sl], in0=s_ap[:], in1=pb[:, 0:TW], op=ALU.mult)),
                      reads=[s_tok, t_pb], writes=[t_hid[j]], wr_nosame=True)
        for n in range(KC):
            wt, wtok = ws.get(wi); wi += 1
            for t in range(NTT):
                sl = slice(t * TW, (t + 1) * TW)
                pd, t_pd = banks[4 + (n * NTT + t) % 2]
                for j in range(ng):
                    P.add("pe", (lambda e, j=j, sl=sl, pd=pd, wt=wt: e.matmul(pd[:, 0:TW], lhsT=wt[:, j, 0:128], rhs=HID[:, j, sl],
                                                                             start=(j == 0), stop=(j == ng - 1))),
                          reads=[wtok, t_hid[j]], writes=[t_pd])
                P.add("dve", (lambda e, n=n, sl=sl, pd=pd: e.scalar_tensor_tensor(out=XT[:, n, sl], in0=pd[:, 0:TW], scalar=0.5, in1=XT[:, n, sl],
                                                                                 op0=ALU.mult, op1=ALU.add)),
                      reads=[t_pd], writes=[t_xt[n]], wr_nosame=True)
    return wi


def ffn_weight_plan(w_gu, w_down):
    plan = []
    gu = w_gu.rearrange("(kc p) n -> p kc n", p=128)
    dn = w_down.rearrange("(fc p) n -> p fc n", p=128)
    for (c0, ng) in FF_GROUPS:
        for j in range(ng):
            m = c0 + j
            plan.append([((lambda tl: tl[:, :, 0:128]), gu[:, :, m * 128:(m + 1) * 128]),
                         ((lambda tl: tl[:, :, 128:256]), gu[:, :, D_FF + m * 128:D_FF + (m + 1) * 128])])
        for n in range(KC):
            plan.append([((lambda tl, ng=ng: tl[:, 0:ng, 0:128]), dn[:, c0:c0 + ng, n * 128:(n + 1) * 128])])
    return plan


def emit_gelu(P, C, out_ap, out_toks, src_ps, t_src, rows, width, idx):
    g1, t_g1 = C["g1"][idx], C["t_g1"][idx]
    g2, t_g2 = C["g2"][idx], C["t_g2"][idx]
    P.add("act", (lambda e: e.activation(out=g1[0:rows, 0:width], in_=src_ps, func=AF.Square)), reads=[t_src], writes=[t_g1])
    P.add("dve", (lambda e: e.tensor_scalar(out=g1[0:rows, 0:width], in0=g1[0:rows, 0:width], scalar1=0.044715, scalar2=1.0, op0=ALU.mult, op1=ALU.add)),
          reads=[t_g1], writes=[t_g1])
    P.add("dve", (lambda e: e.tensor_tensor(out=g1[0:rows, 0:width], in0=g1[0:rows, 0:width], in1=src_ps, op=ALU.mult)), reads=[t_g1, t_src], writes=[t_g1])
    P.add("act", (lambda e: e.activation(out=g2[0:rows, 0:width], in_=g1[0:rows, 0:width], func=AF.Sigmoid, scale=GELU_C)), reads=[t_g1], writes=[t_g2])
    P.add("dve", (lambda e: e.tensor_tensor(out=out_ap, in0=g2[0:rows, 0:width], in1=src_ps, op=ALU.mult)), reads=[t_g2, t_src], writes=out_toks)


def emit_group_rmsnorm_tm(P, C, src_ap, t_src, rows, gain_rep, t_gain, out_f32, t_out, idx):
    junk, t_junk = C["junk"][idx], C["t_junk"][idx]
    ss, t_ss = C["ss"][idx], C["t_ss"][idx]
    for h in range(8):
        P.add("act", (lambda e, h=h: e.activation(out=junk[0:rows, h * 64:(h + 1) * 64], in_=src_ap[:, h * 64:(h + 1) * 64], func=AF.Square,
                                                  accum_out=ss[0:rows, h:h + 1])),
              reads=[t_src], writes=[t_junk, t_ss])
    P.add("act", (lambda e: e.activation(out=ss[0:rows, :], in_=ss[0:rows, :], func=AF.Sqrt, bias=C["eps_col"][0:rows, :], scale=1.0 / 64)),
          reads=[t_ss, C["t_eps"]], writes=[t_ss])
    P.add("dve", (lambda e: e.reciprocal(out=ss[0:rows, :], in_=ss[0:rows, :])), reads=[t_ss], writes=[t_ss])
    P.add("dve", (lambda e: e.tensor_tensor(out=junk[0:rows, :].rearrange("p (h d) -> p h d", h=8), in0=src_ap.rearrange("p (h d) -> p h d", h=8),
                                            in1=ss[0:rows, :].unsqueeze(2).to_broadcast([rows, 8, 64]), op=ALU.mult)),
          reads=[t_src, t_ss], writes=[t_junk])
    P.add("dve", (lambda e: e.tensor_tensor(out=out_f32, in0=junk[0:rows, :], in1=gain_rep[0:rows, :], op=ALU.mult)),
          reads=[t_junk, t_gain], writes=[t_out])


def build_l1(n_pool=N_POOL, stage="full"):
    nc = bass.Bass("TRN2", target_bir_lowering=False)
    din = lambda name, shape, dt=F32: nc.dram_tensor(name, shape, dt, kind="ExternalInput").ap()
    dout = lambda name, shape, dt=F32: nc.dram_tensor(name, shape, dt, kind="ExternalOutput").ap()
    x = din("x", [NT, D_MODEL]); xs = din("xs", [NS, D_MODEL])
    g1T = din("g1T", [128, KC]); gmT = din("gmT", [128, KC])
    w_gu = din("w_gu", [D_MODEL, 2 * D_FF]); w_down = din("w_down", [D_FF, D_MODEL])
    w_ina = din("w_ina", [D_MODEL, 2568])
    w_own = din("w_own", [D_MODEL, 256])
    smalls = din("smalls", [128, 2048])
    consts = din("consts", [128, 384])
    wsT = din("wsT", [128, 8 * 128])
    bsr_in = din("bsr_in", [128, 512])
    eye32_in = din("eye32_in", [1, 1024])
    ptT = din("ptT", [128, NS], I32)
    cache_k = din("cache_k", [n_pool, 8192]); cache_v = din("cache_v", [n_pool, 8192]); cache_lf = din("cache_lf", [n_pool, 128])

    o_k = dout("o_k", [NT, 512]); o_v = dout("o_v", [NT, 512]); o_lf = dout("o_lf", [NT, 8])
    o_ks = dout("o_ks", [NS, 512]); o_vs = dout("o_vs", [NS, 512]); o_lfs = dout("o_lfs", [NS, 8]); o_gvs = dout("o_gvs", [NS, 512])
    o_xt = dout("o_xt", [D_MODEL, NTOK]); o_ht = dout("o_ht", [D_MODEL, NTOK], BF16)
    o_qt = dout("o_qt", [512, NT], BF16); o_kt = dout("o_kt", [512, NT], BF16); o_vb = dout("o_vb", [NT, 512], BF16)
    o_f = dout("o_f", [NT, 8]); o_mt = dout("o_mt", [512, NTOK], BF16)
    o_as = dout("o_as", [64, NS])

    outs = []
    with ExitStack() as st:
        P = Prog(nc, st)
        sb = lambda name, shape, dt: st.enter_context(nc.sbuf_tensor(name, shape, dt))
        psa = lambda name, shape, dt: st.enter_context(nc.psum_tensor(name, shape, dt))

        XTr = sb("XTr", [128, KC * NTOK * 2], BF16)
        XT = XTr[:, :].bitcast(F32).rearrange("p (k t) -> p k t", k=KC); t_xt = [T(f"XT{i}") for i in range(KC)]
        HTr = sb("HTr", [128, KC * NTOK], BF16)
        HT = HTr[:, :].rearrange("p (k t) -> p k t", k=KC); t_ht = [T(f"HT{i}") for i in range(KC)]
        HIDr = sb("HIDr", [128, 8 * NTOK], BF16)
        HID = HIDr[:, :].rearrange("p (k t) -> p k t", k=8); t_hid = [T(f"HID{i}") for i in range(8)]
        cst = sb("cst", [128, 384], F32); t_cst = T("cst")
        sm = sb("sm", [128, 2048], F32); t_sm = T("sm")
        ident = cst[:, 0:128]; tri_le = cst[:, 128:256]; gt = cst[:, 256:384]
        g1s = sm[:, 0:8]; gms = sm[:, 8:16]
        bf_rep = sm[:, 16:24]; qk_col = sm[:, 24:26]; bf_own = sm[:, 26:27]; ws00 = sm[:, 28:32]; bs0 = sm[:, 32:36]
        qg_rep = sm[:, 64:576]; kg_rep = sm[:, 576:1088]; gvg_rep = sm[:, 1088:1600]
        bs_rep = sm[:, 1600:2048 + 64] if False else None
        bsr = sb("bsr", [128, 4, 128], F32); t_bsr = T("bsr")
        ones_bf = sb("ones_bf", [128, 128], BF16); t_ones = T("ones")
        ones_f = sb("ones_f", [128, 128], F32); t_onesf = T("onesf")
        ident_bf = sb("ident_bf", [128, 128], BF16); t_idb = T("idb")
        eps_col = sb("eps_col", [128, 1], F32); t_eps = T("eps")
        sqr = sb("sqr", [128, KC * TW], BF16)
        sq = sqr[:, :].rearrange("p (k t) -> p k t", k=KC)
        rstd = sb("rstd", [128, TW], F32)
        sa = [sb(f"sa{i}", [128, TW], F32) for i in range(2)]
        C = dict(ones_bf=ones_bf, t_ones=t_ones, sq=sq, t_sq=[T() for _ in range(KC)], rstd=rstd, t_rstd=T(), eps_col=eps_col, t_eps=t_eps,
                 sa=sa, t_sa=[T(), T()])
        xin = Slots(P, sb, "xin", 2, [128, D_MODEL], F32)
        ws = WStream(P, sb, "w", [128, 8, 256], nf=2, nb=3)
        banks = [(psa(f"bk{i}", [128, 512], F32), T(f"bk{i}")) for i in range(7)]
        pbf = psa("pbf", [128, 1024], BF16); t_pbf = T("pbf")

        c_c = P.chan(); c_c2 = P.chan()
        P.add("sp", lambda e: e.dma_start(out=cst[:], in_=consts), writes=[t_cst], chan=c_c)
        P.add("sp", lambda e: e.dma_start(out=sm[:], in_=smalls), writes=[t_sm], chan=c_c2)
        P.add("dve", lambda e: e.memset(ones_bf[:], 1.0), writes=[t_ones])
        P.add("dve", lambda e: e.memset(ones_f[:], 1.0), writes=[t_onesf])
        P.add("dve", lambda e: e.memset(eps_col[:], EPS), writes=[t_eps])
        P.add("dve", lambda e: e.tensor_copy(out=ident_bf[:], in_=ident), reads=[t_cst], writes=[t_idb])

        for tt in range(17):
            rows = 128 if tt < 16 else NS
            src = x[tt * 128:(tt + 1) * 128, :] if tt < 16 else xs
            xt_, xtok, xch = xin.next()
            P.add("sp", (lambda e, xt_=xt_, src=src, rows=rows: e.dma_start(out=xt_[0:rows, :], in_=src)), writes=[xtok], chan=xch)
            for half in range(2):
                pt, t_pt = banks[(tt * 2 + half) % 4]
                for k4 in range(4):
                    kc = half * 4 + k4
                    P.add("pe", (lambda e, pt=pt, k4=k4, kc=kc, xt_=xt_, rows=rows: e.transpose(out=pt[:, k4 * 128:k4 * 128 + rows], in_=xt_[0:rows, kc * 128:(kc + 1) * 128],
                                                                                            identity=ident[0:rows, 0:rows])),
                          reads=[xtok, t_cst], writes=[t_pt])
                P.add("act", (lambda e, pt=pt, half=half, tt=tt, rows=rows: e.copy(out=XT[:, half * 4:half * 4 + 4, tt * 128:tt * 128 + rows],
                                                                                 in_=pt[:, :].rearrange("p (k t) -> p k t", k=4)[:, :, 0:rows])),
                      reads=[t_pt], writes=t_xt[half * 4:half * 4 + 4], wr_nosame=True)

        ws.extend(ffn_weight_plan(w_gu, w_down))
        emit_rmsnorm_fm(P, C, XT, t_xt, HT, t_ht, g1s, t_sm, banks[0:2], "n1")
        emit_ffn(P, C, ws, 0, XT, t_xt, HT, t_ht, HID, t_hid, w_gu, w_down, banks[0:6])

        emit_rmsnorm_fm(P, C, XT, t_xt, HT, t_ht, gms, t_sm, banks[0:2], "nm")
        c_xo = P.chan()
        outs.append(P.add("sp", lambda e: e.dma_start(out=o_xt.rearrange("(k p) t -> p k t", p=128), in_=XT), reads=t_xt, chan=c_xo))
        outs.append(P.add("sp", lambda e: e.dma_start(out=o_ht.rearrange("(k p) t -> p k t", p=128), in_=HT), reads=t_ht, chan=c_xo))
        if stage == "A":
            P.run(final_dma_ops=outs)
            return nc

        WINA = XTr[:, 0:KC * 2568].rearrange("p (k n) -> p k n", k=KC); t_wina = T("WINA")
        GV = XTr[:, KC * 2568:KC * 2568 + 17 * 512].rearrange("p (c n) -> p c n", c=17); t_gv = [T(f"GV{i}") for i in range(17)]
        UT = HID[:, 0:4, :]; t_ut = [T(f"UT{i}") for i in range(4)]
        wina_src = w_ina.rearrange("(kc p) n -> p kc n", p=128)
        for i in range(11):
            c0 = i * 256
            cw = 256 if i < 10 else 8
            ft, ftok, fch = ws.f.next()
            P.add("sp", (lambda e, ft=ft, c0=c0, cw=cw: e.dma_start(out=ft[:, :, 0:cw], in_=wina_src[:, :, c0:c0 + cw])), writes=[ftok], chan=fch)
            P.add("pool", (lambda e, ft=ft, c0=c0, cw=cw: e.tensor_copy(out=WINA[:, :, c0:c0 + cw], in_=ft[:, :, 0:cw])), reads=[ftok],
                  writes=[t_wina] + t_xt, wr_nosame=True)
        wsT_sb = sb("wsT_sb", [128, 8, 128], F32); t_wsT = T("wsT")
        wsTm = sb("wsTm", [128, 8, 128], BF16); t_wsTm = T("wsTm")
        c_ws = P.chan(); c_ws2 = P.chan()
        P.add("sp", lambda e: e.dma_start(out=wsT_sb[:], in_=wsT.rearrange("p (g t) -> p g t", g=8)), writes=[t_wsT], chan=c_ws)
        P.add("sp", lambda e: e.dma_start(out=bsr[:], in_=bsr_in.rearrange("p (f t) -> p f t", f=4)), writes=[t_bsr], chan=c_ws2)
        P.add("dve", lambda e: e.tensor_tensor(out=wsTm[:], in0=wsT_sb[:], in1=tri_le.unsqueeze(1).to_broadcast([128, 8, 128]), op=ALU.mult),
              reads=[t_wsT, t_cst], writes=[t_wsTm])

        hi = HIDr[:, 4 * NTOK:4 * NTOK + 8192].bitcast(F32)
        hv = [hi[:, i * 512:(i + 1) * 512] for i in range(8)]
        C["g1"] = [hv[0], hv[1]]; C["t_g1"] = [T(), T()]
        C["g2"] = [hv[2], hv[3]]; C["t_g2"] = [T(), T()]
        C["junk"] = [hv[4], hv[5]]; C["t_junk"] = [T(), T()]
        C["ss"] = [sb(f"ss{i}", [128, 8], F32) for i in range(2)]; C["t_ss"] = [T(), T()]
        kf = Slots(P, sb, "kf", 2, None, F32, tiles=[xin.tiles[0][:, 0:512], xin.tiles[0][:, 512:1024]])
        vf = Slots(P, sb, "vf", 2, None, F32, tiles=[xin.tiles[1][:, 0:512], xin.tiles[1][:, 512:1024]])
        gvf = Slots(P, sb, "gvf", 2, None, F32, tiles=[hv[7], sqr[:, 0:1024].bitcast(F32)])
        qf = Slots(P, sb, "qf", 1, None, F32, with_chan=False, tiles=[hv[6]])
        b16 = Slots(P, sb, "b16", 3, None, BF16, tiles=[sqr[:, 1024 + i * 512:1024 + (i + 1) * 512] for i in range(3)])
        tst = Slots(P, sb, "tst", 2, None, BF16, tiles=[sa[i][:, 0:256].bitcast(BF16).rearrange("p (j t) -> p j t", j=4) for i in range(2)])
        LF = rstd[:, 0:136].rearrange("p (t h) -> p t h", t=17); t_lf = [T(f"LF{i}") for i in range(17)]
        zt = rstd[:, 136:144]; t_zt = T("zt")
        c_lf = P.chan()
        pq, t_pq = banks[0]; pk, t_pk = banks[1]; pv, t_pv = banks[2]; pg, t_pg = banks[3]; pf, t_pf = banks[4]
        o_kt_v = o_kt.rearrange("(j p) t -> p j t", p=128)
        o_qt_v = o_qt.rearrange("(j p) t -> p j t", p=128)

        def transposed_store(src_f32, t_src, rows, tt, dst_view):
            bt, btok, _ = b16.next()
            P.add("act", (lambda e: e.copy(out=bt[0:rows, :], in_=src_f32)), reads=[t_src], writes=[btok])
            for j in range(4):
                P.add("pe", (lambda e, j=j: e.transpose(out=pbf[:, j * 128:j * 128 + rows], in_=bt[0:rows, j * 128:(j + 1) * 128], identity=ident_bf[0:rows, 0:rows])),
                      reads=[btok, t_idb], writes=[t_pbf])
            st_, sttok, stch = tst.next()
            P.add("dve", (lambda e: e.tensor_copy(out=st_[:, :, 0:rows], in_=pbf[:, 0:512].rearrange("p (j t) -> p j t", j=4)[:, :, 0:rows])),
                  reads=[t_pbf], writes=[sttok])
            outs.append(P.add("sp", (lambda e: e.dma_start(out=dst_view[:, :, tt * 128:tt * 128 + rows], in_=st_[:, :, 0:rows])), reads=[sttok], chan=stch))

        def tile_body(tt):
            rows = 128 if tt < 16 else NS
            tsl = slice(tt * 128, tt * 128 + rows)
            rsl = slice(tt * 128, tt * 128 + rows)
            for kc in range(KC):
                for (pp, tp, c0, cw) in ((pq, t_pq, 0, 512), (pk, t_pk, 512, 512), (pv, t_pv, 1024, 512), (pg, t_pg, 1536, 512), (pf, t_pf, 2560, 8)):
                    if tt == 16 and c0 == 0:
                        continue
                    P.add("pe", (lambda e, pp=pp, kc=kc, c0=c0, cw=cw: e.matmul(pp[0:rows, 0:cw], lhsT=HT[:, kc, tsl], rhs=WINA[:, kc, c0:c0 + cw],
                                                                               start=(kc == 0), stop=(kc == KC - 1))),
                          reads=[t_ht[kc], t_wina], writes=[tp])
            i2 = tt % 2
            kt_, ktok, kch = kf.next()
            emit_group_rmsnorm_tm(P, C, pk[0:rows, :], t_pk, rows, kg_rep, t_sm, kt_[0:rows, :], ktok, i2)
            dst = o_k[rsl, :] if tt < 16 else o_ks
            outs.append(P.add("sp", (lambda e, kt_=kt_, dst=dst: e.dma_start(out=dst, in_=kt_[0:rows, :])), reads=[ktok], chan=kch))
            if tt < 16:
                transposed_store(kt_[0:rows, :], ktok, rows, tt, o_kt_v)
            if tt < 16:
                qt_, qtok, _ = qf.next()
                emit_group_rmsnorm_tm(P, C, pq[0:rows, :], t_pq, rows, qg_rep, t_sm, qt_[0:rows, :], qtok, 1 - i2)
                transposed_store(qt_[0:rows, :], qtok, rows, tt, o_qt_v)
            vt_, vtok, vch = vf.next()
            P.add("act", (lambda e, vt_=vt_: e.copy(out=vt_[0:rows, :], in_=pv[0:rows, :])), reads=[t_pv], writes=[vtok])
            dst = o_v[rsl, :] if tt < 16 else o_vs
            outs.append(P.add("sp", (lambda e, vt_=vt_, dst=dst: e.dma_start(out=dst, in_=vt_[0:rows, :])), reads=[vtok], chan=vch))
            if tt < 16:
                bt, btok, bch = b16.next()
                P.add("act", (lambda e, bt=bt, vt_=vt_: e.copy(out=bt[0:rows, :], in_=vt_[0:rows, :])), reads=[vtok], writes=[btok])
                outs.append(P.add("sp", (lambda e, bt=bt: e.dma_start(out=o_vb[rsl, :], in_=bt[0:rows, :])), reads=[btok], chan=bch))
            P.add("dve", (lambda e: e.tensor_tensor(out=zt[0:rows, :], in0=pf[0:rows, 0:8], in1=bf_rep[0:rows, :], op=ALU.add)), reads=[t_pf, t_sm], writes=[t_zt])
            P.add("act", (lambda e: e.activation(out=zt[0:rows, :], in_=zt[0:rows, :], func=AF.Exp, scale=-1.0)), reads=[t_zt], writes=[t_zt])
            P.add("act", (lambda e: e.activation(out=zt[0:rows, :], in_=zt[0:rows, :], func=AF.Ln, bias=ones_f[0:rows, 0:1], scale=1.0)), reads=[t_zt, t_onesf], writes=[t_zt])
            P.add("dve", (lambda e, tt=tt: e.tensor_scalar(out=LF[0:rows, tt, :], in0=zt[0:rows, :], scalar1=-1.0, scalar2=None, op0=ALU.mult)),
                  reads=[t_zt], writes=[t_lf[tt]])
            dst = o_lf[rsl, :] if tt < 16 else o_lfs
            outs.append(P.add("sp", (lambda e, tt=tt, dst=dst: e.dma_start(out=dst, in_=LF[0:rows, tt, :])), reads=[t_lf[tt]], chan=c_lf))
            gtmp, t_gtmp = C["g2"][1 - i2], C["t_g2"][1 - i2]
            emit_gelu(P, C, gtmp[0:rows, :], [t_gtmp], pg[0:rows, :], t_pg, rows, 512, i2)
            gt_, gtok, gch = gvf.next()
            emit_group_rmsnorm_tm(P, C, gtmp[0:rows, :], t_gtmp, rows, gvg_rep, t_sm, gt_[0:rows, :], gtok, i2)
            if tt == 16:
                outs.append(P.add("sp", (lambda e, gt_=gt_: e.dma_start(out=o_gvs, in_=gt_[0:rows, :])), reads=[gtok], chan=gch))
            P.add("act", (lambda e, gt_=gt_, tt=tt: e.copy(out=GV[0:rows, tt, :], in_=gt_[0:rows, :])), reads=[gtok], writes=[t_gv[tt]])

        for tt in range(17):
            tile_body(tt)

        FS = rstd[:, 144:272].rearrange("p (t h) -> p t h", t=16); t_fs = T("FS")
        pF, t_pF = banks[4]
        for tt in range(16):
            P.add("pe", (lambda e, tt=tt: e.matmul(pF[:, tt * 8:(tt + 1) * 8], lhsT=tri_le, rhs=LF[:, tt, :], start=True, stop=(tt == 0))),
                  reads=[t_cst, t_lf[tt]], writes=[t_pF])
            for j in range(tt):
                P.add("pe", (lambda e, tt=tt, j=j: e.matmul(pF[:, tt * 8:(tt + 1) * 8], lhsT=ones_f[:], rhs=LF[:, j, :], start=False, stop=(j == tt - 1))),
                      reads=[t_onesf, t_lf[j]], writes=[t_pF])
        P.add("act", lambda e: e.copy(out=FS, in_=pF[:, 0:128].rearrange("p (t h) -> p t h", t=16)), reads=[t_pF], writes=[t_fs])
        c_fs = P.chan()
        outs.append(P.add("sp", lambda e: e.dma_start(out=o_f.rearrange("(t p) h -> p t h", p=128), in_=FS), reads=[t_fs], chan=c_fs))

        for m in range(4):
            for t in range(NTT):
                sl = slice(t * TW, (t + 1) * TW)
                pu, t_pu = banks[5 + (m * NTT + t) % 2]
                for kc in range(KC):
                    P.add("pe", (lambda e, kc=kc, m=m, sl=sl, pu=pu: e.matmul(pu[:, 0:TW], lhsT=WINA[:, kc, 2048 + m * 128:2048 + (m + 1) * 128], rhs=HT[:, kc, sl],
                                                                             start=(kc == 0), stop=(kc == KC - 1))),
                          reads=[t_wina, t_ht[kc]], writes=[t_pu])
                emit_gelu(P, C, UT[:, m, sl], [t_ut[m]], pu[:, 0:TW], t_pu, 128, TW, (m * NTT + t) % 2)

        gtmp2 = sb("gtmp2", [128, 2, 128], F32); t_gt2 = [T(), T()]
        for c in range(16):
            csl = slice(c * 128, (c + 1) * 128)
            for fc in range(4):
                par = (c * 4 + fc) % 2
                pG, t_pG = banks[5 + par]
                P.add("pe", (lambda e, c=c, fc=fc, pG=pG: e.matmul(pG[:, 0:128], lhsT=GV[:, c, fc * 128:(fc + 1) * 128], rhs=wsTm[:, 2 * fc, :], start=True, stop=True)),
                      reads=[t_gv[c], t_wsTm], writes=[t_pG])
                P.add("pe", (lambda e, c=c, fc=fc, pG=pG: e.matmul(pG[:, 128:256], lhsT=GV[:, c, fc * 128:(fc + 1) * 128], rhs=wsTm[:, 2 * fc + 1, :], start=True, stop=True)),
                      reads=[t_gv[c], t_wsTm], writes=[t_pG])
                P.add("dve", (lambda e, fc=fc, pG=pG, par=par: e.tensor_tensor(out=gtmp2[0:64, par, :], in0=pG[0:64, 0:128], in1=bsr[0:64, fc, :], op=ALU.add)),
                      reads=[t_pG, t_bsr], writes=[t_gt2[par]])
                P.add("dve", (lambda e, fc=fc, pG=pG, par=par: e.tensor_tensor(out=gtmp2[64:128, par, :], in0=pG[64:128, 128:256], in1=bsr[64:128, fc, :], op=ALU.add)),
                      reads=[t_pG, t_bsr], writes=[t_gt2[par]], wr_nosame=True)
                P.add("dve", (lambda e, fc=fc, csl=csl, par=par: e.tensor_tensor(out=UT[:, fc, csl], in0=gtmp2[:, par, :], in1=UT[:, fc, csl], op=ALU.mult)),
                      reads=[t_gt2[par]], writes=[t_ut[fc]], wr_nosame=True)
        for fc in range(4):
            P.add("pe", (lambda e, fc=fc: e.transpose(out=pbf[:, fc * 32:(fc + 1) * 32], in_=GV[0:NS, 16, fc * 128:(fc + 1) * 128], identity=ident_bf[0:NS, 0:NS])),
                  reads=[t_gv[16], t_idb], writes=[t_pbf])
        for fc in range(4):
            P.add("dve", (lambda e, fc=fc: e.tensor_scalar(out=gtmp2[:, 0, 0:NS], in0=pbf[:, fc * 32:(fc + 1) * 32], scalar1=ws00[:, fc:fc + 1], scalar2=bs0[:, fc:fc + 1],
                                                          op0=ALU.mult, op1=ALU.add)),
                  reads=[t_pbf, t_sm], writes=[t_gt2[0]])
            P.add("dve", (lambda e, fc=fc: e.tensor_tensor(out=UT[:, fc, NT:NTOK], in0=gtmp2[:, 0, 0:NS], in1=UT[:, fc, NT:NTOK], op=ALU.mult)),
                  reads=[t_gt2[0], t_ut[fc]], writes=[t_ut[fc]])
        c_mt = P.chan()
        outs.append(P.add("sp", lambda e: e.dma_start(out=o_mt.rearrange("(m p) t -> p m t", p=128), in_=UT), reads=t_ut, chan=c_mt))
        if stage == "B":
            P.run(final_dma_ops=outs)
            return nc

        MULENG = "dve"; REDENG = "dve"
        f0 = ws.f.tiles[0][:, :, :].rearrange("p a b -> p (a b)")
        lft = f0[:, 0:128]; lfT = f0[:, 128:256]; stile = f0[:, 256:384]; Pm = f0[:, 384:512]
        red = f0[:, 512:576]; qb = f0[:, 576:640]; qrep = f0[0:64, 640:768]
        Pmb = f0[:, 384:448].bitcast(BF16)
        orow = ws.f.tiles[1][:, :, :].rearrange("p a b -> p (a b)")[0:1, 0:2048]; t_orow = T("orow")
        eye32 = sm[0:1, 1600:2624] if False else None
        Tp = f0[:, 768:769]; bcol = f0[:, 769:770]; rs = f0[:, 770:771]
        sqs = f0[0:64, 784:880]; qkn = f0[0:64, 880:944]; sq2 = f0[0:64, 944:1008]; rq = f0[0:64, 1008:1072]
        frow = f0[0:1, 1072:1104]; FNB = f0[:, 1104:1136]; PN64 = f0[0:64, 1136:1168]; ASs = f0[0:64, 1168:1200]; DNs = f0[0:64, 1200:1232]
        prod = f0[0:64, 1232:1264]
        t_lft = T(); t_lfT = T(); t_st = T(); t_Pm = T(); t_red = T(); t_qb = T(); t_qrep = T(); t_Tp = T(); t_bcol = T(); t_rs = T()
        t_sqs = T(); t_qkn = T(); t_sq2 = T(); t_rq = T(); t_frow = T(); t_FNB = T(); t_PN = T(); t_ASs = T(); t_DNs = T(); t_prod = T()
        ptsb = sb("ptsb", [128, NS], I32); t_pts = T("pts")
        c_pt = P.chan(); c_e32 = P.chan()
        P.add("sp", lambda e: e.dma_start(out=ptsb[:], in_=ptT), writes=[t_pts], chan=c_pt)
        eye32r = f0[0:1, 1280:1280 + 1024] if False else sb("eye32r", [1, 1024], F32); t_e32 = T("e32")
        P.add("sp", lambda e: e.dma_start(out=eye32r[:], in_=eye32_in), writes=[t_e32], chan=c_e32)
        wown = ws.b.tiles[0]; t_wown = ws.b.toks[0]
        ft1, ftok1, fch1 = ws.f.tiles[1], ws.f.toks[1], ws.f.chans[1]
        P.add("sp", lambda e: e.dma_start(out=ft1[:, :, :], in_=w_own.rearrange("(kc p) n -> p kc n", p=128)), writes=[ftok1], chan=fch1)
        P.add("pool", lambda e: e.tensor_copy(out=wown[:, :, :], in_=ft1[:, :, :]), reads=[ftok1], writes=[t_wown])
        po, t_po = banks[5]
        for gi, (c0, m) in enumerate(((0, 64), (64, 64), (128, 64), (192, 1))):
            for kc in range(KC):
                P.add("pe", (lambda e, gi=gi, c0=c0, m=m, kc=kc: e.matmul(po[0:m, gi * 32:(gi + 1) * 32], lhsT=wown[:, kc, c0:c0 + m], rhs=HT[:, kc, NT:NTOK],
                                                                         start=(kc == 0), stop=(kc == KC - 1))),
                      reads=[t_wown, t_ht[kc]], writes=[t_po])
        P.add("act", lambda e: e.copy(out=sqs, in_=po[0:64, 0:96]), reads=[t_po], writes=[t_sqs])
        P.add("act", lambda e: e.copy(out=frow, in_=po[0:1, 96:128]), reads=[t_po], writes=[t_frow])
        P.add("dve", lambda e: e.tensor_tensor(out=sq2, in0=sqs[:, 0:64], in1=sqs[:, 0:64], op=ALU.mult), reads=[t_sqs], writes=[t_sq2])
        pS, t_pS = banks[6]
        P.add("pe", lambda e: e.matmul(pS[0:64, 0:64], lhsT=ones_f[0:64, 0:64], rhs=sq2, start=True, stop=True), reads=[t_sq2, t_onesf], writes=[t_pS])
        P.add("act", lambda e: e.activation(out=rq, in_=pS[0:64, 0:64], func=AF.Sqrt, bias=eps_col[0:64, :], scale=1.0 / 64), reads=[t_pS, t_eps], writes=[t_rq])
        P.add("dve", lambda e: e.reciprocal(out=rq, in_=rq), reads=[t_rq], writes=[t_rq])
        P.add("dve", lambda e: e.scalar_tensor_tensor(out=qkn[:, 0:32], in0=sqs[:, 0:32], scalar=qk_col[0:64, 0:1], in1=rq[:, 0:32], op0=ALU.mult, op1=ALU.mult),
              reads=[t_sqs, t_rq, t_sm], writes=[t_qkn])
        P.add("dve", lambda e: e.tensor_scalar(out=qkn[:, 0:32], in0=qkn[:, 0:32], scalar1=0.125, scalar2=None, op0=ALU.mult), reads=[t_qkn], writes=[t_qkn])
        P.add("dve", lambda e: e.scalar_tensor_tensor(out=qkn[:, 32:64], in0=sqs[:, 32:64], scalar=qk_col[0:64, 1:2], in1=rq[:, 32:64], op0=ALU.mult, op1=ALU.mult),
              reads=[t_sqs, t_rq, t_sm], writes=[t_qkn])
        P.add("dve", lambda e: e.tensor_tensor(out=prod, in0=qkn[:, 0:32], in1=qkn[:, 32:64], op=ALU.mult), reads=[t_qkn], writes=[t_prod])
        P.add("pe", lambda e: e.matmul(pS[0:64, 64:96], lhsT=ones_f[0:64, 0:64], rhs=prod, start=True, stop=True), reads=[t_prod, t_onesf], writes=[t_pS])
        P.add("act", lambda e: e.activation(out=PN64, in_=pS[0:64, 64:96], func=AF.Exp), reads=[t_pS], writes=[t_PN])
        P.add("dve", lambda e: e.tensor_scalar(out=frow, in0=frow, scalar1=bf_own[0:1, :], scalar2=None, op0=ALU.add), reads=[t_frow, t_sm], writes=[t_frow])
        P.add("act", lambda e: e.activation(out=frow, in_=frow, func=AF.Exp, scale=-1.0), reads=[t_frow], writes=[t_frow])
        P.add("act", lambda e: e.activation(out=frow, in_=frow, func=AF.Ln, bias=ones_f[0:1, 0:1], scale=1.0), reads=[t_frow, t_onesf], writes=[t_frow])
        P.add("dve", lambda e: e.tensor_scalar(out=frow, in0=frow, scalar1=-1.0, scalar2=None, op0=ALU.mult), reads=[t_frow], writes=[t_frow])
        P.add("pe", lambda e: e.matmul(pS[:, 96:128], lhsT=ones_f[0:1, 0:128], rhs=frow, start=True, stop=True), reads=[t_frow, t_onesf], writes=[t_pS])
        P.add("act", lambda e: e.copy(out=FNB, in_=pS[:, 96:128]), reads=[t_pS], writes=[t_FNB])

        Kslots = [XTr[:, 0:16384].bitcast(F32)]; Vslots = [XTr[:, 16384:32768].bitcast(F32)]
        t_Ks = [T("K0"), T("K1")]; t_Vs = [T("V0"), T("V1")]
        c_ks = [P.chan(), P.chan()]; c_vs = [P.chan(), P.chan()]; c_l = P.chan()
        dead = [t_wina] + t_gv + t_xt
        ck4 = cache_k.rearrange("n (a c) -> (n a) c", a=4); cv4 = cache_v.rearrange("n (a c) -> (n a) c", a=4)
        ptx = sb("ptx", [128, 4, NS], I32); t_ptx = T("ptx")
        for a in range(4):
            P.add("dve", (lambda e, a=a: e.tensor_scalar(out=ptx[:, a, :], in0=ptsb[:], scalar1=4, scalar2=a, op0=ALU.mult, op1=ALU.add)),
                  reads=[t_pts], writes=[t_ptx], wr_nosame=True)
        pQ = pS[:, 128:192]; pT = pS[:, 192:320]; pW = pS[:, 320:448]; pSp = pS[:, 448:449]
        pAS = pS[0:64, 450:482]; pDN = pS[0:64, 482:514] if False else None
        pB, t_pB = banks[5]
        pAS = pB[0:64, 128:160]; pDN = pB[0:64, 160:192]
        t_pQ = T(); t_pT = T(); t_pW = T(); t_pSp = T(); t_pAS = T(); t_pDN = T()

        def sample_body(b):
            sb_ = 0
            Ks, Vs, t_K, t_V, c_k, c_v = Kslots[sb_], Vslots[sb_], t_Ks[sb_], t_Vs[sb_], c_ks[sb_], c_vs[sb_]
            for a in range(4):
                P.add("pool", (lambda e, a=a: e.indirect_dma_start(out=Ks[:, a * 2048:(a + 1) * 2048], out_offset=None, in_=ck4,
                                                                   in_offset=bass.IndirectOffsetOnAxis(ap=ptx[:, a, b:b + 1], axis=0))),
                      reads=[t_ptx], writes=[t_K] + (dead if b < 1 else []), chan=c_k, wr_nosame=True)
            for a in range(4):
                P.add("pool", (lambda e, a=a: e.indirect_dma_start(out=Vs[:, a * 2048:(a + 1) * 2048], out_offset=None, in_=cv4,
                                                                   in_offset=bass.IndirectOffsetOnAxis(ap=ptx[:, a, b:b + 1], axis=0))),
                      reads=[t_ptx], writes=[t_V] + (dead if b < 1 else []), chan=c_v, wr_nosame=True)
            P.add("pool", (lambda e: e.indirect_dma_start(out=lft, out_offset=None, in_=cache_lf,
                                                          in_offset=bass.IndirectOffsetOnAxis(ap=ptsb[:, b:b + 1], axis=0))),
                  reads=[t_pts], writes=[t_lft], chan=c_l)
            P.add("dve", (lambda e: e.tensor_copy(out=qrep, in_=qkn[:, b:b + 1].to_broadcast([64, 128]))), reads=[t_qkn], writes=[t_qrep])
            P.add("pe", (lambda e: e.matmul(pQ, lhsT=qrep, rhs=ident[0:64, 0:64], start=True, stop=True)), reads=[t_qrep, t_cst], writes=[t_pQ])
            P.add("act", (lambda e: e.copy(out=qb, in_=pQ)), reads=[t_pQ], writes=[t_qb])
            K3 = Ks.rearrange("p (s d) -> p s d", d=64)
            P.add(MULENG, (lambda e: e.tensor_tensor(out=K3, in0=K3, in1=qb.unsqueeze(1).to_broadcast([128, 128, 64]), op=ALU.mult)), reads=[t_K, t_qb], writes=[t_K])
            P.add(REDENG, (lambda e: e.tensor_reduce(out=stile, in_=K3, axis=AX.X, op=ALU.add)), reads=[t_K], writes=[t_st])
            P.add("pe", (lambda e: e.transpose(out=pT, in_=lft, identity=ident)), reads=[t_lft, t_cst], writes=[t_pT])
            P.add("act", (lambda e: e.copy(out=lfT, in_=pT)), reads=[t_pT], writes=[t_lfT])
            P.add("pe", (lambda e: e.matmul(pW, lhsT=lfT, rhs=gt, start=True, stop=True)), reads=[t_lfT, t_cst], writes=[t_pW])
            P.add("dve", (lambda e: e.tensor_reduce(out=Tp, in_=lft, axis=AX.X, op=ALU.add)), reads=[t_lft], writes=[t_Tp])
            P.add("pe", (lambda e: e.matmul(pSp, lhsT=gt, rhs=Tp, start=True, stop=True)), reads=[t_Tp, t_cst], writes=[t_pSp])
            P.add("dve", (lambda e: e.tensor_tensor(out=bcol, in0=pSp, in1=FNB[:, b:b + 1], op=ALU.add)), reads=[t_pSp, t_FNB], writes=[t_bcol])
            P.add("dve", (lambda e: e.tensor_tensor(out=stile, in0=stile, in1=pW, op=ALU.add)), reads=[t_st, t_pW], writes=[t_st])
            P.add("act", (lambda e: e.activation(out=Pm, in_=stile, func=AF.Exp, bias=bcol, scale=1.0, accum_out=rs)), reads=[t_st, t_bcol], writes=[t_Pm, t_rs])
            V3 = Vs.rearrange("p (s d) -> p s d", d=64)
            P.add(MULENG, (lambda e: e.tensor_tensor(out=V3, in0=V3, in1=Pm.unsqueeze(2).to_broadcast([128, 128, 64]), op=ALU.mult)), reads=[t_V, t_Pm], writes=[t_V])
            P.add(REDENG, (lambda e: e.tensor_reduce(out=red, in_=Vs.rearrange("p (s d) -> p d s", d=64), axis=AX.X, op=ALU.add)), reads=[t_V], writes=[t_red])
            P.add("pe", (lambda e: e.matmul(pAS[:, b:b + 1], lhsT=red, rhs=ones_f[:, 0:1], start=True, stop=True)), reads=[t_red, t_onesf], writes=[t_pAS])
            P.add("pe", (lambda e: e.matmul(pDN[:, b:b + 1], lhsT=ones_f[:, 0:64], rhs=rs, start=True, stop=True)), reads=[t_rs, t_onesf], writes=[t_pDN])

        for b in range(NS):
            sample_body(b)
        P.add("dve", lambda e: e.tensor_tensor(out=ASs, in0=PN64, in1=sqs[:, 64:96], op=ALU.mult), reads=[t_PN, t_sqs], writes=[t_ASs])
        P.add("dve", lambda e: e.tensor_tensor(out=ASs, in0=ASs, in1=pAS, op=ALU.add), reads=[t_ASs, t_pAS], writes=[t_ASs])
        P.add("dve", lambda e: e.tensor_tensor(out=DNs, in0=PN64, in1=pDN, op=ALU.add), reads=[t_PN, t_pDN], writes=[t_DNs])
        P.add("dve", lambda e: e.reciprocal(out=DNs, in_=DNs), reads=[t_DNs], writes=[t_DNs])
        P.add("dve", lambda e: e.tensor_tensor(out=ASs, in0=ASs, in1=DNs, op=ALU.mult), reads=[t_ASs, t_DNs], writes=[t_ASs])
        c_as = P.chan()
        outs.append(P.add("sp", lambda e: e.dma_start(out=o_as, in_=ASs), reads=[t_ASs], chan=c_as))
        P.run(final_dma_ops=outs)
    return nc


def build_l2(stage="full"):
    nc = bass.Bass("TRN2", target_bir_lowering=False)
    din = lambda name, shape, dt=F32: nc.dram_tensor(name, shape, dt, kind="ExternalInput").ap()
    dout = lambda name, shape, dt=F32: nc.dram_tensor(name, shape, dt, kind="ExternalOutput").ap()
    i_xt = din("i_xt", [D_MODEL, NTOK]); i_ht = din("i_ht", [D_MODEL, NTOK], BF16)
    i_qt = din("i_qt", [512, NT], BF16); i_kt = din("i_kt", [512, 2 * NT], BF16); i_vb = din("i_vb", [2 * NT, 512], BF16)
    i_f = din("i_f", [2 * NT, 8]); i_mt = din("i_mt", [512, NTOK], BF16); i_as = din("i_as", [512, NS])
    w_ing = din("w_ing", [D_MODEL, 2048]); w_pa = din("w_pa", [512, D_MODEL]); w_pg = din("w_pg", [512, D_MODEL]); w_o = din("w_o", [D_MODEL, D_MODEL])
    w_gu = din("w_gu", [D_MODEL, 2 * D_FF]); w_down = din("w_down", [D_FF, D_MODEL])
    w_plg = din("w_plg", [D_MODEL, D_MODEL]); w_plp = din("w_plp", [256, D_MODEL])
    pin = din("pin", [NT, 256]); pins = din("pins", [NS, 256])
    smalls = din("smalls", [128, 64]); consts = din("consts", [128, 384])
    o_y = dout("o_y", [NT, D_MODEL]); o_ys = dout("o_ys", [NS, D_MODEL])
    o_at = dout("o_at", [512, NTOK], BF16) if stage in ("A", "M") else None
    o_dbb = dout("o_dbb", [D_MODEL, NTOK], BF16) if stage in ("M",) else None
    o_dbf = dout("o_dbf", [D_MODEL, NTOK], F32) if stage in ("X2", "X3") else None

    outs = []
    with ExitStack() as st:
        P = Prog(nc, st)
        sb = lambda name, shape, dt: st.enter_context(nc.sbuf_tensor(name, shape, dt))
        psa = lambda name, shape, dt: st.enter_context(nc.psum_tensor(name, shape, dt))
        XTr = sb("XTr", [128, KC * NTOK * 2], BF16)
        XT = XTr[:, :].bitcast(F32).rearrange("p (k t) -> p k t", k=KC); t_xt = [T(f"XT{i}") for i in range(KC)]
        HTr = sb("HTr", [128, KC * NTOK], BF16)
        HT = HTr[:, :].rearrange("p (k t) -> p k t", k=KC); t_ht = [T(f"HT{i}") for i in range(KC)]
        HIDr = sb("HIDr", [128, 8 * NTOK], BF16)
        HID = HIDr[:, :].rearrange("p (k t) -> p k t", k=8); t_hid = [T(f"HID{i}") for i in range(8)]
        AT = sb("AT", [128, 4, NTOK], BF16); t_at = [T(f"AT{i}") for i in range(4)]
        MTr = sb("MTr", [128, 4 * NTOK], BF16)
        MT = MTr[:, :].rearrange("p (k t) -> p k t", k=4); t_mt = T("MT")
        cst = sb("cst", [128, 384], F32); t_cst = T("cst")
        sm = sb("sm", [128, 64], F32); t_sm = T("sm")
        ident = cst[:, 0:128]; tri_le = cst[:, 128:256]
        g2s = sm[:, 0:8]; gps = sm[:, 8:16]; pbias = sm[:, 16:17]
        ones_bf = sb("ones_bf", [128, 128], BF16); t_ones = T("ones")
        tri_bf = sb("tri_bf", [128, 128], BF16); t_tri = T("tri")
        eps_col = sb("eps_col", [128, 1], F32); t_eps = T("eps")
        sqr = sb("sqr", [128, KC * TW], BF16)
        sq = sqr[:, :].rearrange("p (k t) -> p k t", k=KC)
        rstd = sb("rstd", [128, TW], F32)
        sa = [sb(f"sa{i}", [128, TW], F32) for i in range(2)]
        C = dict(ones_bf=ones_bf, t_ones=t_ones, sq=sq, t_sq=[T() for _ in range(KC)], rstd=rstd, t_rstd=T(), eps_col=eps_col, t_eps=t_eps,
                 sa=sa, t_sa=[T(), T()])
        ws = WStream(P, sb, "w", [128, 8, 256], nf=2, nb=3, depth=1)
        banks = [(psa(f"bk{i}", [128, 512], F32), T(f"bk{i}")) for i in range(8)]

        c_c = P.chan(); c_c2 = P.chan()
        P.add("sp", lambda e: e.dma_start(out=cst[:], in_=consts), writes=[t_cst], chan=c_c)
        P.add("sp", lambda e: e.dma_start(out=sm[:], in_=smalls), writes=[t_sm], chan=c_c2)
        P.add("dve", lambda e: e.memset(ones_bf[:], 1.0), writes=[t_ones])
        P.add("dve", lambda e: e.memset(eps_col[:], EPS), writes=[t_eps])
        P.add("dve", lambda e: e.tensor_copy(out=tri_bf[:], in_=tri_le), reads=[t_cst], writes=[t_tri])
        c_in = P.chan(); c_in2 = P.chan()
        P.add("sp", lambda e: e.dma_start(out=HT, in_=i_ht.rearrange("(k p) t -> p k t", p=128)), writes=t_ht, chan=c_in)
        P.add("sp", lambda e: e.dma_start(out=MT, in_=i_mt.rearrange("(k p) t -> p k t", p=128)), writes=[t_mt], chan=c_in2)
        c_as = P.chan()
        P.add("pool", lambda e: e.dma_start(out=AT[:, :, NT:NTOK], in_=i_as.rearrange("(k p) t -> p k t", p=128)), writes=t_at, chan=c_as)

        xr = XTr
        off = [0]

        def carve(nel, dt=BF16):
            a = xr[:, off[0]:off[0] + nel]
            off[0] += nel
            return a if dt == BF16 else a.bitcast(dt)
        KTs = [carve(4096) for _ in range(2)]; t_kts = [T(), T()]
        QTs = [carve(2048) for _ in range(2)]; t_qts = [T(), T()]
        Vld = [carve(32 * 128).rearrange("p (k n) -> p k n", k=32) for _ in range(1)]; t_vld = [T()]
        VA = [carve(32 * 128).rearrange("p (k n) -> p k n", k=32) for _ in range(2)]; t_va = [T(), T()]
        Pt = [carve(512) for _ in range(4)]; t_ptl = [T(), T(), T(), T()]
        Fk = carve(32 * 8 * 2, F32).rearrange("p (k h) -> p k h", k=32); t_fk = T()
        Fref = carve(5 * 8 * 2, F32).rearrange("p (q h) -> p q h", q=5); t_fref = T()
        Bm = carve(32 * 4 * 8 * 2, F32).rearrange("p (k q h) -> p k q h", k=32, q=4); t_bm = T()
        rden = carve(512 * 2, F32); t_rden = T()
        c_k = [P.chan(), P.chan()]; c_q = [P.chan(), P.chan()]; c_v = P.chan(); c_f = P.chan()
        P.add("sp", lambda e: e.dma_start(out=Fk, in_=i_f.rearrange("(k p) h -> p k h", p=128)), writes=[t_fk], chan=c_f)
        for qt in range(4):
            r = NT + qt * 512
            P.add("sp", (lambda e, qt=qt, r=r: e.dma_start(out=Fref[:, qt, :], in_=i_f[r:r + 1, :].partition_broadcast(128))), writes=[t_fref], chan=c_f, wr_nosame=True)
        P.add("sp", lambda e: e.dma_start(out=Fref[:, 4, :], in_=i_f[NT - 1:NT, :].partition_broadcast(128)), writes=[t_fref], chan=c_f, wr_nosame=True)
        P.add("pool", lambda e: e.memset(VA[0][:, :, 0:64], 1.0), writes=[t_va[0]])
        P.add("pool", lambda e: e.memset(VA[1][:, :, 64:128], 1.0), writes=[t_va[1]])
        P.add("dve", lambda e: e.tensor_tensor(out=Fk[:, 0:16, :], in0=Fk[:, 0:16, :], in1=Fref[:, 4:5, :].to_broadcast([128, 16, 8]), op=ALU.subtract),
              reads=[t_fk, t_fref], writes=[t_fk])
        P.add("dve", lambda e: e.tensor_scalar(out=Fk[:, 0:16, :], in0=Fk[:, 0:16, :], scalar1=pbias, scalar2=None, op0=ALU.subtract), reads=[t_fk, t_sm], writes=[t_fk])
        for qt in range(4):
            P.add("dve", (lambda e, qt=qt: e.tensor_tensor(out=Bm[:, :, qt, :], in0=Fref[:, qt:qt + 1, :].to_broadcast([128, 32, 8]), in1=Fk, op=ALU.subtract)),
                  reads=[t_fk, t_fref], writes=[t_bm], wr_nosame=True)

        o_at_v = o_at.rearrange("(k p) t -> p k t", p=128) if o_at is not None else None

        def attn_chunk(c):
            par = c % 2
            kt_, qt_ = KTs[par], QTs[par]
            P.add("sp", (lambda e: e.dma_start(out=kt_, in_=i_kt[c * 128:(c + 1) * 128, :])), writes=[t_kts[par]], chan=c_k[par])
            P.add("sp", (lambda e: e.dma_start(out=qt_, in_=i_qt[c * 128:(c + 1) * 128, :])), writes=[t_qts[par]], chan=c_q[par])
            P.add("sp", (lambda e: e.dma_start(out=Vld[0], in_=i_vb[:, c * 128:(c + 1) * 128].rearrange("(k p) n -> p k n", p=128))), writes=[t_vld[0]], chan=c_v)
            for hl in range(2):
                P.add("dve", (lambda e, hl=hl: e.tensor_copy(out=VA[hl][:, :, (1 - hl) * 64:(2 - hl) * 64], in_=Vld[0][:, :, hl * 64:(hl + 1) * 64])), reads=[t_vld[0]], writes=[t_va[hl]], wr_nosame=True)
            for hl in range(2):
                h = 2 * c + hl
                prt = slice(hl * 64, (hl + 1) * 64)
                for qt in range(4):
                    pA, t_pA = banks[4 + (h * 4 + qt) % 2]
                    kts = list(range(16)) + [16 + ko for ko in range(4 * qt + 4)]
                    n = len(kts)
                    LA = 2
                    info = {}
                    for i in range(n + LA):
                        if i < n:
                            kt = kts[i]
                            ko = kt - 16
                            j0 = 0
                            if ko >= 0 and ko * 128 > qt * 512:
                                j0 = ko * 128 - qt * 512
                            diag = ko >= 0 and ko * 128 >= qt * 512
                            w = 512 - j0
                            q0 = qt * 512 + j0
                            pS, t_pS = banks[i % 4]
                            pt_, t_pt = Pt[i % 4], t_ptl[i % 4]
                            P.add("pe", (lambda e, pS=pS, kt=kt, q0=q0, w=w, prt=prt: e.matmul(pS[:, 0:w], lhsT=kt_[prt, kt * 128:(kt + 1) * 128], rhs=qt_[prt, q0:q0 + w], start=True, stop=True)),
                                  reads=[t_kts[par], t_qts[par]], writes=[t_pS])
                            P.add("act", (lambda e, pS=pS, pt_=pt_, kt=kt, qt=qt, h=h, w=w: e.activation(out=pt_[:, 0:w], in_=pS[:, 0:w], func=AF.Exp, bias=Bm[:, kt, qt, h:h + 1], scale=0.125)),
                                  reads=[t_pS, t_bm], writes=[t_pt])
                            if diag:
                                P.add("dve", (lambda e, pt_=pt_: e.tensor_tensor(out=pt_[:, 0:128], in0=pt_[:, 0:128], in1=tri_le, op=ALU.mult)), reads=[t_pt, t_cst], writes=[t_pt])
                            info[i] = (pt_, t_pt, kt, j0, w)
                        j = i - LA
                        if j >= 0:
                            ppt, ptok, pkt, pj0, pw = info.pop(j)
                            P.add("pe", (lambda e, pA=pA, ppt=ppt, pkt=pkt, pj0=pj0, pw=pw, j=j, n=n, hl=hl: e.matmul(pA[:, pj0:512], lhsT=VA[hl][:, pkt, :], rhs=ppt[:, 0:pw],
                                                                                                               start=(j == 0), stop=(j == n - 1))),
                                  reads=[t_va[hl], ptok], writes=[t_pA])
                    dsl = slice(hl * 64, (hl + 1) * 64)
                    nsl = slice((1 - hl) * 64, (2 - hl) * 64)
                    P.add("dve", (lambda e, pA=pA, dsl=dsl: e.reciprocal(out=rden[dsl, :], in_=pA[dsl, :])), reads=[t_pA], writes=[t_rden])
                    P.add("dve", (lambda e, pA=pA, qt=qt, dsl=dsl, nsl=nsl: e.tensor_tensor(out=AT[dsl, c, qt * 512:(qt + 1) * 512], in0=pA[nsl, :], in1=rden[dsl, :], op=ALU.mult)),
                          reads=[t_pA, t_rden], writes=[t_at[c]], wr_nosame=True)

        for c in range(4):
            attn_chunk(c)
        c_ao = P.chan()
        if stage == "A":
            outs.append(P.add("sp", lambda e: e.dma_start(out=o_at_v, in_=AT[:]), reads=t_at, chan=c_ao))
            P.run(final_dma_ops=outs)
            return nc

        attn_toks = t_kts + t_qts + t_vld + t_va + t_ptl + [t_fk, t_fref, t_bm, t_rden]
        sa2 = [sqr[:, i * 832:(i + 1) * 832].bitcast(F32) for i in range(2)]; t_sa2 = [T(), T()]
        plan = []
        ging = w_ing.rearrange("(kc p) n -> p kc n", p=128)
        gpa = w_pa.rearrange("(kc p) n -> p kc n", p=128); gpg = w_pg.rearrange("(kc p) n -> p kc n", p=128)
        gwo = w_o.rearrange("(kc p) n -> p kc n", p=128)
        for m in range(8):
            plan.append([((lambda tl: tl[:, :, 0:128]), ging[:, :, m * 128:(m + 1) * 128]),
                         ((lambda tl: tl[:, :, 128:256]), ging[:, :, 1024 + m * 128:1024 + (m + 1) * 128])])
            plan.append([((lambda tl: tl[:, 0:4, 0:128]), gpa[:, :, m * 128:(m + 1) * 128]),
                         ((lambda tl: tl[:, 4:8, 0:128]), gpg[:, :, m * 128:(m + 1) * 128])])
        for n in range(8):
            plan.append([((lambda tl: tl[:, :, 0:128]), gwo[:, :, n * 128:(n + 1) * 128])])
        ws.extend(plan)
        ffn_base = len(ws.plan)
        ws.extend(ffn_weight_plan(w_gu, w_down))
        ple_base = len(ws.plan)
        gplg = w_plg.rearrange("(kc p) n -> p kc n", p=128); gplp = w_plp.rearrange("(kc p) n -> p kc n", p=128)
        plan = []
        for n in range(8):
            plan.append([((lambda tl: tl[:, :, 0:128]), gplg[:, :, n * 128:(n + 1) * 128]),
                         ((lambda tl: tl[:, 0:2, 128:256]), gplp[:, :, n * 128:(n + 1) * 128])])
        ws.extend(plan)

        def merge_m(m):
            wg, wgtok = ws.get(2 * m)
            wp, wptok = ws.get(2 * m + 1)
            for t in range(NTT):
                sl = slice(t * TW, (t + 1) * TW)
                par = (m * NTT + t) % 2
                (ga, t_ga), (gb, t_gb), (pa, t_pa), (pb, t_pb) = banks[4 * par:4 * par + 4]
                for kc in range(KC):
                    P.add("pe", (lambda e, kc=kc, sl=sl, ga=ga: e.matmul(ga[:, 0:TW], lhsT=wg[:, kc, 0:128], rhs=HT[:, kc, sl], start=(kc == 0), stop=(kc == KC - 1))),
                          reads=[wgtok, t_ht[kc]], writes=[t_ga])
                for kc in range(KC):
                    P.add("pe", (lambda e, kc=kc, sl=sl, gb=gb: e.matmul(gb[:, 0:TW], lhsT=wg[:, kc, 128:256], rhs=HT[:, kc, sl], start=(kc == 0), stop=(kc == KC - 1))),
                          reads=[wgtok, t_ht[kc]], writes=[t_gb])
                for k4 in range(4):
                    P.add("pe", (lambda e, k4=k4, sl=sl, pa=pa: e.matmul(pa[:, 0:TW], lhsT=wp[:, k4, 0:128], rhs=AT[:, k4, sl], start=(k4 == 0), stop=(k4 == 3))),
                          reads=[wptok, t_at[k4]], writes=[t_pa])
                for k4 in range(4):
                    P.add("pe", (lambda e, k4=k4, sl=sl, pb=pb: e.matmul(pb[:, 0:TW], lhsT=wp[:, 4 + k4, 0:128], rhs=MT[:, k4, sl], start=(k4 == 0), stop=(k4 == 3))),
                          reads=[wptok, t_mt], writes=[t_pb])
                sA, t_sA = (sa[par], C["t_sa"][par])
                sB, t_sB = (sa2[par], t_sa2[par])
                P.add("act", (lambda e, ga=ga, sA=sA: e.activation(out=sA[:], in_=ga[:, 0:TW], func=AF.Sigmoid)), reads=[t_ga], writes=[t_sA])
                P.add("act", (lambda e, gb=gb, sB=sB: e.activation(out=sB[:], in_=gb[:, 0:TW], func=AF.Sigmoid)), reads=[t_gb], writes=[t_sB] + C["t_sq"], wr_nosame=True)
                P.add("dve", (lambda e, pa=pa, sA=sA: e.tensor_tensor(out=sA[:], in0=sA[:], in1=pa[:, 0:TW], op=ALU.mult)), reads=[t_sA, t_pa], writes=[t_sA])
                P.add("dve", (lambda e, pb=pb, sB=sB: e.tensor_tensor(out=sB[:], in0=sB[:], in1=pb[:, 0:TW], op=ALU.mult)), reads=[t_sB, t_pb], writes=[t_sB])
                P.add("dve", (lambda e, sA=sA, sB=sB, sl=sl: e.tensor_tensor(out=HID[:, m, sl], in0=sA[:], in1=sB[:], op=ALU.add)), reads=[t_sA, t_sB], writes=[t_hid[m]], wr_nosame=True)

        for m in range(8):
            merge_m(m)

        if stage == "M":
            outs.append(P.add("sp", lambda e: e.dma_start(out=o_dbb.rearrange("(k p) t -> p k t", p=128), in_=HID), reads=t_hid, chan=c_ao))
            outs.append(P.add("sp", lambda e: e.dma_start(out=o_at_v, in_=AT[:]), reads=t_at, chan=c_ao))
            P.run(final_dma_ops=outs)
            return nc
        c_x = [P.chan() for _ in range(8)]

        def wout_n(n):
            wo, wotok = ws.get(16 + n)
            P.add("sp", (lambda e: e.dma_start(out=XT[:, n, :], in_=i_xt[n * 128:(n + 1) * 128, :])), writes=[t_xt[n]] + attn_toks, chan=c_x[n], wr_nosame=True)
            for t in range(NTT):
                sl = slice(t * TW, (t + 1) * TW)
                po, t_po = banks[(n * NTT + t) % 2]
                for m in range(8):
                    P.add("pe", (lambda e, m=m, sl=sl, po=po: e.matmul(po[:, 0:TW], lhsT=wo[:, m, 0:128], rhs=HID[:, m, sl], start=(m == 0), stop=(m == 7))),
                          reads=[wotok, t_hid[m]], writes=[t_po])
                P.add("dve", (lambda e, sl=sl, po=po: e.tensor_tensor(out=XT[:, n, sl], in0=XT[:, n, sl], in1=po[:, 0:TW], op=ALU.add)),
                      reads=[t_po, t_xt[n]], writes=[t_xt[n]], wr_nosame=True)

        for n in range(8):
            wout_n(n)

        if stage == "X2":
            outs.append(P.add("sp", lambda e: e.dma_start(out=o_dbf.rearrange("(k p) t -> p k t", p=128), in_=XT), reads=t_xt, chan=c_ao))
            P.run(final_dma_ops=outs)
            return nc
        emit_rmsnorm_fm(P, C, XT, t_xt, HT, t_ht, g2s, t_sm, banks[0:2], "n2")
        emit_ffn(P, C, ws, ffn_base, XT, t_xt, HT, t_ht, HID, t_hid, w_gu, w_down, banks[0:6])
        if stage == "X3":
            outs.append(P.add("sp", lambda e: e.dma_start(out=o_dbf.rearrange("(k p) t -> p k t", p=128), in_=XT), reads=t_xt, chan=c_ao))
            P.run(final_dma_ops=outs)
            return nc

        emit_rmsnorm_fm(P, C, XT, t_xt, HT, t_ht, gps, t_sm, banks[0:2], "np")
        PT = AT[:, 0:2, :]
        pslots = Slots(P, sb, "pin", 2, None, F32, tiles=[MTr[:, i * 512:(i + 1) * 512].bitcast(F32) for i in range(2)])
        for tt in range(17):
            rows = 128 if tt < 16 else NS
            src = pin[tt * 128:(tt + 1) * 128, :] if tt < 16 else pins
            pt_, ptok, pch = pslots.next()
            P.add("sp", (lambda e, pt_=pt_, src=src, rows=rows: e.dma_start(out=pt_[0:rows, :], in_=src)), writes=[ptok, t_mt], chan=pch, wr_nosame=True)
            pp, t_pp = banks[6 + tt % 2]
            for k2 in range(2):
                P.add("pe", (lambda e, pp=pp, k2=k2, pt_=pt_, rows=rows: e.transpose(out=pp[:, k2 * 128:k2 * 128 + rows], in_=pt_[0:rows, k2 * 128:(k2 + 1) * 128], identity=ident[0:rows, 0:rows])),
                      reads=[ptok, t_cst], writes=[t_pp])
            P.add("act", (lambda e, pp=pp, tt=tt, rows=rows: e.copy(out=PT[:, :, tt * 128:tt * 128 + rows], in_=pp[:, 0:256].rearrange("p (k t) -> p k t", k=2)[:, :, 0:rows])),
                  reads=[t_pp], writes=[t_at[0], t_at[1]], wr_nosame=True)

        def ple_n(n):
            wl, wltok = ws.get(ple_base + n)
            for t in range(NTT):
                sl = slice(t * TW, (t + 1) * TW)
                par = (n * NTT + t) % 2
                pg_, t_pg_ = banks[2 * par]
                pw_, t_pw_ = banks[2 * par + 1]
                for kc in range(KC):
                    P.add("pe", (lambda e, kc=kc, sl=sl, pg_=pg_: e.matmul(pg_[:, 0:TW], lhsT=wl[:, kc, 0:128], rhs=HT[:, kc, sl], start=(kc == 0), stop=(kc == KC - 1))),
                          reads=[wltok, t_ht[kc]], writes=[t_pg_])
                for k2 in range(2):
                    P.add("pe", (lambda e, k2=k2, sl=sl, pw_=pw_: e.matmul(pw_[:, 0:TW], lhsT=wl[:, k2, 128:256], rhs=PT[:, k2, sl], start=(k2 == 0), stop=(k2 == 1))),
                          reads=[wltok, t_at[k2]], writes=[t_pw_])
                sA, t_sA = (sa[par], C["t_sa"][par])
                P.add("act", (lambda e, pg_=pg_, sA=sA: e.activation(out=sA[:], in_=pg_[:, 0:TW], func=AF.Sigmoid)), reads=[t_pg_], writes=[t_sA])
                P.add("dve", (lambda e, pw_=pw_, sA=sA: e.tensor_tensor(out=sA[:], in0=sA[:], in1=pw_[:, 0:TW], op=ALU.mult)), reads=[t_sA, t_pw_], writes=[t_sA])
                P.add("dve", (lambda e, sA=sA, sl=sl: e.tensor_tensor(out=XT[:, n, sl], in0=XT[:, n, sl], in1=sA[:], op=ALU.add)), reads=[t_sA, t_xt[n]], writes=[t_xt[n]], wr_nosame=True)

        for n in range(8):
            ple_n(n)

        yst = Slots(P, sb, "yst", 2, None, F32, tiles=[HIDr[:, i * 2048:(i + 1) * 2048].bitcast(F32) for i in range(2)])
        for tt in range(17):
            rows = 128 if tt < 16 else NS
            y_, ytok, ych = yst.next()
            for half in range(2):
                pt, t_pt = banks[(tt * 2 + half) % 4]
                for k4 in range(4):
                    kc = half * 4 + k4
                    P.add("pe", (lambda e, pt=pt, k4=k4, kc=kc, tt=tt, rows=rows: e.transpose(out=pt[0:rows, k4 * 128:(k4 + 1) * 128], in_=XT[:, kc, tt * 128:tt * 128 + rows], identity=ident)),
                          reads=[t_xt[kc], t_cst], writes=[t_pt])
                P.add("act", (lambda e, pt=pt, half=half, y_=y_, rows=rows: e.copy(out=y_[0:rows, half * 512:(half + 1) * 512], in_=pt[0:rows, :])),
                      reads=[t_pt], writes=[ytok] + t_hid, wr_nosame=True)
            dst = o_y[tt * 128:(tt + 1) * 128, :] if tt < 16 else o_ys
            outs.append(P.add("sp", (lambda e, y_=y_, dst=dst, rows=rows: e.dma_start(out=dst, in_=y_[0:rows, :])), reads=[ytok], chan=ych))
        P.run(final_dma_ops=outs)
    return nc


_NC_CACHE = {}


def _consts():
    return np.concatenate([np.eye(128), np.triu(np.ones((128, 128))), np.tril(np.ones((128, 128)), -1)], 1).astype(np.float32)


def _gT(g):
    return np.ascontiguousarray(np.asarray(g, np.float32).reshape(8, 128).T)


def kernel(x_prompt, x_sample, cache_k, cache_v, cache_logf, page_table, p_prompt, p_sample,
           ffn1_norm, ffn1_w_gu, ffn1_w_down, mix_norm, w_in, b_forget, q_norm, k_norm,
           gmlp_v_norm, w_spatial, b_spatial, w_proj_attn, w_proj_gmlp, w_out,
           ffn2_norm, ffn2_w_gu, ffn2_w_down, ple_norm, ple_w_gate, ple_w_proj):
    f32 = lambda a: np.ascontiguousarray(np.asarray(a, np.float32))
    x_prompt = f32(x_prompt); x_sample = f32(x_sample)
    cache_k = np.asarray(cache_k); cache_v = np.asarray(cache_v); cache_logf = np.asarray(cache_logf)
    page_table = np.asarray(page_table).astype(np.int32)
    wi = f32(w_in)[0]; bfg = f32(b_forget)[0]; qn = f32(q_norm)[0]; kn = f32(k_norm)[0]; gvn = f32(gmlp_v_norm)[0]
    wsp = f32(w_spatial)[0]; bsp = f32(b_spatial)[0]
    n_pool = cache_k.shape[1]
    consts = _consts()
    w_ina = np.ascontiguousarray(np.concatenate([wi[:, 0:1536], wi[:, 2056:2568], wi[:, 1544:2056], wi[:, 1536:1544]], 1))
    wsT = np.ascontiguousarray(wsp.transpose(2, 0, 1)).reshape(128, 1024)
    bsr = np.zeros((128, 4, 128), np.float32)
    for fc in range(4):
        for gl in range(2):
            bsr[gl * 64:(gl + 1) * 64, fc, :] = bsp[2 * fc + gl][None, :]
    w_gu1 = f32(ffn1_w_gu)[0]; w_dn1 = f32(ffn1_w_down)[0]
    ptT = np.ascontiguousarray(page_table.T)
    xs = np.ascontiguousarray(x_sample[:, 0, :])

    in1 = []
    for c in range(8):
        b, half = c // 2, c % 2
        sm = np.zeros((128, 2048), np.float32)
        sm[:, 0:8] = _gT(f32(ffn1_norm)[0]); sm[:, 8:16] = _gT(f32(mix_norm)[0])
        sm[:, 16:24] = bfg[None, :]
        sm[0:64, 24] = qn; sm[0:64, 25] = kn
        sm[:, 26] = bfg[c]
        for fc in range(4):
            for gl in range(2):
                sm[gl * 64:(gl + 1) * 64, 28 + fc] = wsp[2 * fc + gl, 0, 0]
                sm[gl * 64:(gl + 1) * 64, 32 + fc] = bsp[2 * fc + gl, 0]
        sm[:, 64:576] = np.tile(qn, 8)[None, :]
        sm[:, 576:1088] = np.tile(kn, 8)[None, :]
        sm[:, 1088:1600] = np.tile(gvn, 8)[None, :]
        w_own = np.zeros((1024, 256), np.float32)
        w_own[:, 0:64] = wi[:, c * 64:(c + 1) * 64]; w_own[:, 64:128] = wi[:, 512 + c * 64:512 + (c + 1) * 64]
        w_own[:, 128:192] = wi[:, 1024 + c * 64:1024 + (c + 1) * 64]; w_own[:, 192] = wi[:, 1536 + c]
        in1.append(dict(
            x=np.ascontiguousarray(x_prompt[b, half * NT:(half + 1) * NT]), xs=xs,
            g1T=_gT(f32(ffn1_norm)[0]), gmT=_gT(f32(mix_norm)[0]), w_gu=w_gu1, w_down=w_dn1, w_ina=w_ina, w_own=w_own,
            smalls=sm, consts=consts, wsT=wsT, bsr_in=bsr.reshape(128, 512), ptT=ptT, eye32_in=np.eye(32, dtype=np.float32).reshape(1, 1024),
            cache_k=np.ascontiguousarray(cache_k[0, :, :, c, :], dtype=np.float32).reshape(n_pool, 8192),
            cache_v=np.ascontiguousarray(cache_v[0, :, :, c, :], dtype=np.float32).reshape(n_pool, 8192),
            cache_lf=np.ascontiguousarray(cache_logf[0, :, :, c], dtype=np.float32)))
    key1 = ("l1", n_pool)
    if key1 not in _NC_CACHE:
        _NC_CACHE[key1] = build_l1(n_pool=n_pool)
    r1 = run_bass_kernel_spmd(_NC_CACHE[key1], in1, core_ids=list(range(8))).results
    del in1

    a_all = np.ascontiguousarray(np.concatenate([np.asarray(r1[h]["o_as"], np.float32) for h in range(8)], 0))
    w_ing = np.ascontiguousarray(wi[:, 2568:4616])
    w_gu2 = f32(ffn2_w_gu)[0]; w_dn2 = f32(ffn2_w_down)[0]
    w_pa = f32(w_proj_attn)[0]; w_pg = f32(w_proj_gmlp)[0]; w_o = f32(w_out)[0]; w_plg = f32(ple_w_gate)[0]; w_plp = f32(ple_w_proj)[0]
    p_prompt = f32(p_prompt); pins = np.ascontiguousarray(f32(p_sample)[0, :, 0, :])
    bf = ml_dtypes.bfloat16
    in2 = []
    for c in range(8):
        b, half = c // 2, c % 2
        sm = np.zeros((128, 64), np.float32)
        sm[:, 0:8] = _gT(f32(ffn2_norm)[0]); sm[:, 8:16] = _gT(f32(ple_norm)[0])
        sm[:, 16] = 0.0 if half == 1 else NEG
        own = r1[c]
        if half == 1:
            pr = r1[c - 1]
            kt_prior = np.asarray(pr["o_kt"]); vb_prior = np.asarray(pr["o_vb"]); f_prior = np.asarray(pr["o_f"], np.float32)
        else:
            kt_prior = np.zeros((512, NT), bf); vb_prior = np.zeros((NT, 512), bf); f_prior = np.zeros((NT, 8), np.float32)
        in2.append(dict(
            i_xt=np.asarray(own["o_xt"], np.float32), i_ht=np.asarray(own["o_ht"]), i_qt=np.asarray(own["o_qt"]),
            i_kt=np.ascontiguousarray(np.concatenate([kt_prior, np.asarray(own["o_kt"])], 1)),
            i_vb=np.ascontiguousarray(np.concatenate([vb_prior, np.asarray(own["o_vb"])], 0)),
            i_f=np.ascontiguousarray(np.concatenate([f_prior, np.asarray(own["o_f"], np.float32)], 0)),
            i_mt=np.asarray(own["o_mt"]), i_as=a_all,
            w_ing=w_ing, w_pa=w_pa, w_pg=w_pg, w_o=w_o, w_gu=w_gu2, w_down=w_dn2, w_plg=w_plg, w_plp=w_plp,
            pin=np.ascontiguousarray(p_prompt[0, b, half * NT:(half + 1) * NT]), pins=pins, smalls=sm, consts=consts))
    if "l2" not in _NC_CACHE:
        _NC_CACHE["l2"] = build_l2()
    r2 = run_bass_kernel_spmd(_NC_CACHE["l2"], in2, core_ids=list(range(8))).results

    B, S = 4, 4096
    y_prompt = np.zeros((B, S, D_MODEL), np.float32)
    k_p = np.zeros((1, B, S, 8, 64), np.float32); v_p = np.zeros((1, B, S, 8, 64), np.float32); lf_p = np.zeros((1, B, S, 8), np.float32)
    for c in range(8):
        b, half = c // 2, c % 2
        sl = slice(half * NT, (half + 1) * NT)
        y_prompt[b, sl] = np.asarray(r2[c]["o_y"], np.float32)
        k_p[0, b, sl] = np.asarray(r1[c]["o_k"], np.float32).reshape(NT, 8, 64)
        v_p[0, b, sl] = np.asarray(r1[c]["o_v"], np.float32).reshape(NT, 8, 64)
        lf_p[0, b, sl] = np.asarray(r1[c]["o_lf"], np.float32)
    y_sample = np.asarray(r2[0]["o_ys"], np.float32).reshape(NS, 1, D_MODEL)
    k_s = np.asarray(r1[0]["o_ks"], np.float32).reshape(1, NS, 1, 8, 64)
    v_s = np.asarray(r1[0]["o_vs"], np.float32).reshape(1, NS, 1, 8, 64)
    lf_s = np.asarray(r1[0]["o_lfs"], np.float32).reshape(1, NS, 1, 8)
    gv_s = np.asarray(r1[0]["o_gvs"], np.float32).reshape(1, NS, 1, 8, 64)
    return (y_prompt, y_sample, k_p, v_p, lf_p, k_s, v_s, lf_s, gv_s)
```

```python
import numpy as np
from contextlib import ExitStack
import ml_dtypes
import concourse.bass as bass
import concourse.mybir as mybir
from concourse.bass_utils import run_bass_kernel_spmd

F32 = mybir.dt.float32
BF16 = mybir.dt.bfloat16
I32 = mybir.dt.int32
ALU = mybir.AluOpType
AF = mybir.ActivationFunctionType
AX = mybir.AxisListType

D_MODEL = 1024
KC = 8
NT = 2048
NS = 32
NTOK = NT + NS
TW = 416
NTT = 5
D_FF = 2816
NFC = 22
FF_GROUPS = [(0, 8), (8, 8), (16, 6)]
N_POOL = 5120
EPS = 1e-6
GELU_C = 1.5957691216057308
NEG = -30000.0

SAME_ENG_SYNC = True


class T:
    __slots__ = ("name", "last_w", "readers")

    def __init__(self, name=""):
        self.name = name
        self.last_w = None
        self.readers = []


class Chan:
    def __init__(self, sem):
        self.sem = sem
        self.count = 0


class Op:
    __slots__ = ("eng", "fn", "waits", "needs_sig", "is_dma", "chan", "sig_val", "inc")

    def __init__(self, eng, fn, is_dma, chan, inc):
        self.eng = eng
        self.fn = fn
        self.waits = []
        self.needs_sig = False
        self.is_dma = is_dma
        self.chan = chan
        self.sig_val = None
        self.inc = inc


class Prog:
    ENGS = ("pe", "act", "dve", "pool", "sp")

    def __init__(self, nc, stack):
        self.nc = nc
        self.stack = stack
        self.ops = {e: [] for e in self.ENGS}
        self.csem = {e: stack.enter_context(nc.semaphore("c_" + e)) for e in self.ENGS}
        self.nchan = 0

    def chan(self, name=None):
        self.nchan += 1
        s = self.stack.enter_context(self.nc.semaphore(name or f"d{self.nchan}"))
        return Chan(s)

    def add(self, eng, fn, reads=(), writes=(), chan=None, inc=16, wr_nosame=False):
        is_dma = chan is not None
        op = Op(eng, fn, is_dma, chan, inc)
        deps = {}
        for t in reads:
            if t.last_w is not None:
                deps[id(t.last_w)] = (t.last_w, "raw")
        for t in writes:
            if t.last_w is not None and id(t.last_w) not in deps:
                deps[id(t.last_w)] = (t.last_w, "waw")
            for r in t.readers:
                if id(r) not in deps:
                    deps[id(r)] = (r, "war")
        for d, kind in deps.values():
            if d.eng == eng and kind == "waw" and wr_nosame and d.is_dma == is_dma:
                continue
            if d.eng == eng and not d.is_dma and not is_dma:
                if eng == "pe" or kind == "war" or not SAME_ENG_SYNC:
                    continue
            d.needs_sig = True
            op.waits.append(d)
        for t in reads:
            t.readers.append(op)
        for t in writes:
            t.last_w = op
            t.readers = []
        if is_dma:
            chan.count += inc
            op.sig_val = chan.count
            op.needs_sig = True
        self.ops[eng].append(op)
        return op

    def _emit_engine(self, e, engobj):
        waited = {}
        for op in self.ops[e]:
            for d in op.waits:
                if d.is_dma:
                    sem, val = d.chan.sem, d.sig_val
                else:
                    sem, val = self.csem[d.eng], d.sig_val
                k = id(sem)
                if waited.get(k, 0) >= val:
                    continue
                waited[k] = val
                engobj.wait_ge(sem, val)
            inst = op.fn(engobj)
            if op.needs_sig:
                if op.is_dma:
                    inst.then_inc(op.chan.sem, op.inc)
                else:
                    inst.then_inc(self.csem[e], 1)

    def run(self, final_dma_ops=()):
        for e in self.ENGS:
            c = 0
            for op in self.ops[e]:
                if not op.is_dma and op.needs_sig:
                    c += 1
                    op.sig_val = c
        best = {}
        for d in final_dma_ops:
            k = id(d.chan.sem)
            if k not in best or best[k][1] < d.sig_val:
                best[k] = (d.chan.sem, d.sig_val)
        nc = self.nc
        with nc.Block() as block:
            @block.tensor
            def _(eng):
                self._emit_engine("pe", eng)

            @block.scalar
            def _(eng):
                self._emit_engine("act", eng)

            @block.vector
            def _(eng):
                self._emit_engine("dve", eng)

            @block.gpsimd
            def _(eng):
                self._emit_engine("pool", eng)

            @block.sync
            def _(eng):
                self._emit_engine("sp", eng)
                for sem, val in best.values():
                    eng.wait_ge(sem, val)


class Slots:
    def __init__(self, P, alloc, name, n, shape, dt, with_chan=True, tiles=None):
        self.tiles = tiles if tiles is not None else [alloc(f"{name}{i}", shape, dt) for i in range(n)]
        self.toks = [T(f"{name}{i}") for i in range(n)]
        self.chans = [P.chan() for i in range(n)] if with_chan else [None] * n
        self.n = n
        self.i = -1

    def next(self):
        self.i = (self.i + 1) % self.n
        return self.tiles[self.i], self.toks[self.i], self.chans[self.i]


class WStream:
    def __init__(self, P, alloc, name, shape, nf=2, nb=3, depth=None):
        self.P = P
        self.f = Slots(P, alloc, name + "f", nf, shape, F32)
        self.b = Slots(P, alloc, name + "b", nb, shape, BF16, with_chan=False)
        self.plan = []
        self.issued = 0
        self.depth = (nb - 1) if depth is None else depth

    def extend(self, items):
        self.plan.extend(items)

    def _issue(self, i):
        P = self.P
        parts = self.plan[i]
        ft, ftok, fch = self.f.next()
        for dsts, src in parts:
            P.add("sp", (lambda e, dsts=dsts, src=src, ft=ft: e.dma_start(out=dsts(ft), in_=src)), writes=[ftok], chan=fch, wr_nosame=True)
        bt, btok, _ = self.b.next()
        for dsts, src in parts:
            P.add("pool", (lambda e, dsts=dsts, ft=ft, bt=bt: e.tensor_copy(out=dsts(bt), in_=dsts(ft))), reads=[ftok], writes=[btok], wr_nosame=True)
        return bt, btok

    def get(self, i):
        while self.issued <= min(i + self.depth, len(self.plan) - 1):
            self.ready = getattr(self, "ready", {})
            self.ready[self.issued] = self._issue(self.issued)
            self.issued += 1
        return self.ready.pop(i)


def emit_rmsnorm_fm(P, C, XT, t_xt, HT, t_ht, gT, t_g, psum_tiles, name):
    sq, t_sq = C["sq"], C["t_sq"]
    rstd, t_rstd = C["rstd"], C["t_rstd"]
    for t in range(NTT):
        sl = slice(t * TW, (t + 1) * TW)
        ps, t_ps = psum_tiles[t % len(psum_tiles)]
        for kc in range(KC):
            P.add("act", (lambda e, kc=kc, sl=sl: e.activation(out=sq[:, kc, :], in_=XT[:, kc, sl], func=AF.Square)),
                  reads=[t_xt[kc]], writes=[t_sq[kc]])
        for kc in range(KC):
            P.add("pe", (lambda e, kc=kc, ps=ps: e.matmul(ps[:, 0:TW], lhsT=C["ones_bf"][:], rhs=sq[:, kc, :], start=(kc == 0), stop=(kc == KC - 1))),
                  reads=[t_sq[kc], C["t_ones"]], writes=[t_ps])
        P.add("act", (lambda e, ps=ps: e.activation(out=rstd[:], in_=ps[:, 0:TW], func=AF.Sqrt, bias=C["eps_col"][:], scale=1.0 / D_MODEL)),
              reads=[t_ps, C["t_eps"]], writes=[t_rstd])
        P.add("dve", (lambda e: e.reciprocal(out=rstd[:], in_=rstd[:])), reads=[t_rstd], writes=[t_rstd])
        for kc in range(KC):
            P.add("dve", (lambda e, kc=kc, sl=sl: e.scalar_tensor_tensor(out=HT[:, kc, sl], in0=XT[:, kc, sl], scalar=gT[:, kc:kc + 1],
                                                                         in1=rstd[:], op0=ALU.mult, op1=ALU.mult)),
                  reads=[t_xt[kc], t_g, t_rstd], writes=[t_ht[kc]], wr_nosame=True)


def emit_ffn(P, C, ws, wbase, XT, t_xt, HT, t_ht, HID, t_hid, w_gu, w_down, banks):
    sa, t_sa = C["sa"], C["t_sa"]
    wi = wbase
    for (c0, ng) in FF_GROUPS:
        for j in range(ng):
            wt, wtok = ws.get(wi); wi += 1
            for t in range(NTT):
                sl = slice(t * TW, (t + 1) * TW)
                par = (j * NTT + t) % 2
                pa, t_pa = banks[2 * par]
                pb, t_pb = banks[2 * par + 1]
                for kc in range(KC):
                    P.add("pe", (lambda e, kc=kc, sl=sl, pa=pa, wt=wt: e.matmul(pa[:, 0:TW], lhsT=wt[:, kc, 0:128], rhs=HT[:, kc, sl],
                                                                               start=(kc == 0), stop=(kc == KC - 1))),
                          reads=[wtok, t_ht[kc]], writes=[t_pa])
                for kc in range(KC):
                    P.add("pe", (lambda e, kc=kc, sl=sl, pb=pb, wt=wt: e.matmul(pb[:, 0:TW], lhsT=wt[:, kc, 128:256], rhs=HT[:, kc, sl],
                                                                               start=(kc == 0), stop=(kc == KC - 1))),
                          reads=[wtok, t_ht[kc]], writes=[t_pb])
                s_ap, s_tok = sa[par], t_sa[par]
                P.add("act", (lambda e, pa=pa, s_ap=s_ap: e.activation(out=s_ap[:], in_=pa[:, 0:TW], func=AF.Silu)),
                      reads=[t_pa], writes=[s_tok])
                P.add("dve", (lambda e, pb=pb, s_ap=s_ap, j=j, sl=sl: e.tensor_tensor(out=HID[:, j, sl], in0=s_ap[:], in1=pb[:, 0:TW], op=ALU.mult)),
                      reads=[s_tok, t_pb], writes=[t_hid[j]], wr_nosame=True)
        for n in range(KC):
            wt, wtok = ws.get(wi); wi += 1
            for t in range(NTT):
                sl = slice(t * TW, (t + 1) * TW)
                pd, t_pd = banks[4 + (n * NTT + t) % 2]
                for j in range(ng):
                    P.add("pe", (lambda e, j=j, sl=sl, pd=pd, wt=wt: e.matmul(pd[:, 0:TW], lhsT=wt[:, j, 0:128], rhs=HID[:, j, sl],
                                                                             start=(j == 0), stop=(j == ng - 1))),
                          reads=[wtok, t_hid[j]], writes=[t_pd])
                P.add("dve", (lambda e, n=n, sl=sl, pd=pd: e.scalar_tensor_tensor(out=XT[:, n, sl], in0=pd[:, 0:TW], scalar=0.5, in1=XT[:, n, sl],
                                                                                 op0=ALU.mult, op1=ALU.add)),
                      reads=[t_pd], writes=[t_xt[n]], wr_nosame=True)
    return wi


def ffn_weight_plan(w_gu, w_down):
    plan = []
    gu = w_gu.rearrange("(kc p) n -> p kc n", p=128)
    dn = w_down.rearrange("(fc p) n -> p fc n", p=128)
    for (c0, ng) in FF_GROUPS:
        for j in range(ng):
            m = c0 + j
            plan.append([((lambda tl: tl[:, :, 0:128]), gu[:, :, m * 128:(m + 1) * 128]),
                         ((lambda tl: tl[:, :, 128:256]), gu[:, :, D_FF + m * 128:D_FF + (m + 1) * 128])])
        for n in range(KC):
            plan.append([((lambda tl, ng=ng: tl[:, 0:ng, 0:128]), dn[:, c0:c0 + ng, n * 128:(n + 1) * 128])])
    return plan


def emit_gelu(P, C, out_ap, out_toks, src_ps, t_src, rows, width, idx):
    g1, t_g1 = C["g1"][idx], C["t_g1"][idx]
    g2, t_g2 = C["g2"][idx], C["t_g2"][idx]
    P.add("act", (lambda e: e.activation(out=g1[0:rows, 0:width], in_=src_ps, func=AF.Square)), reads=[t_src], writes=[t_g1])
    P.add("dve", (lambda e: e.tensor_scalar(out=g1[0:rows, 0:width], in0=g1[0:rows, 0:width], scalar1=0.044715, scalar2=1.0, op0=ALU.mult, op1=ALU.add)),
          reads=[t_g1], writes=[t_g1])
    P.add("dve", (lambda e: e.tensor_tensor(out=g1[0:rows, 0:width], in0=g1[0:rows, 0:width], in1=src_ps, op=ALU.mult)), reads=[t_g1, t_src], writes=[t_g1])
    P.add("act", (lambda e: e.activation(out=g2[0:rows, 0:width], in_=g1[0:rows, 0:width], func=AF.Sigmoid, scale=GELU_C)), reads=[t_g1], writes=[t_g2])
    P.add("dve", (lambda e: e.tensor_tensor(out=out_ap, in0=g2[0:rows, 0:width], in1=src_ps, op=ALU.mult)), reads=[t_g2, t_src], writes=out_toks)


def emit_group_rmsnorm_tm(P, C, src_ap, t_src, rows, gain_rep, t_gain, out_f32, t_out, idx):
    junk, t_junk = C["junk"][idx], C["t_junk"][idx]
    ss, t_ss = C["ss"][idx], C["t_ss"][idx]
    for h in range(8):
        P.add("act", (lambda e, h=h: e.activation(out=junk[0:rows, h * 64:(h + 1) * 64], in_=src_ap[:, h * 64:(h + 1) * 64], func=AF.Square,
                                                  accum_out=ss[0:rows, h:h + 1])),
              reads=[t_src], writes=[t_junk, t_ss])
    P.add("act", (lambda e: e.activation(out=ss[0:rows, :], in_=ss[0:rows, :], func=AF.Sqrt, bias=C["eps_col"][0:rows, :], scale=1.0 / 64)),
          reads=[t_ss, C["t_eps"]], writes=[t_ss])
    P.add("dve", (lambda e: e.reciprocal(out=ss[0:rows, :], in_=ss[0:rows, :])), reads=[t_ss], writes=[t_ss])
    P.add("dve", (lambda e: e.tensor_tensor(out=junk[0:rows, :].rearrange("p (h d) -> p h d", h=8), in0=src_ap.rearrange("p (h d) -> p h d", h=8),
                                            in1=ss[0:rows, :].unsqueeze(2).to_broadcast([rows, 8, 64]), op=ALU.mult)),
          reads=[t_src, t_ss], writes=[t_junk])
    P.add("dve", (lambda e: e.tensor_tensor(out=out_f32, in0=junk[0:rows, :], in1=gain_rep[0:rows, :], op=ALU.mult)),
          reads=[t_junk, t_gain], writes=[t_out])


def build_l1(n_pool=N_POOL, stage="full"):
    nc = bass.Bass("TRN2", target_bir_lowering=False)
    din = lambda name, shape, dt=F32: nc.dram_tensor(name, shape, dt, kind="ExternalInput").ap()
    dout = lambda name, shape, dt=F32: nc.dram_tensor(name, shape, dt, kind="ExternalOutput").ap()
    x = din("x", [NT, D_MODEL]); xs = din("xs", [NS, D_MODEL])
    g1T = din("g1T", [128, KC]); gmT = din("gmT", [128, KC])
    w_gu = din("w_gu", [D_MODEL, 2 * D_FF]); w_down = din("w_down", [D_FF, D_MODEL])
    w_ina = din("w_ina", [D_MODEL, 2568])
    w_own = din("w_own", [D_MODEL, 256])
    smalls = din("smalls", [128, 2048])
    consts = din("consts", [128, 384])
    wsT = din("wsT", [128, 8 * 128])
    bsr_in = din("bsr_in", [128, 512])
    eye32_in = din("eye32_in", [1, 1024])
    ptT = din("ptT", [128, NS], I32)
    cache_k = din("cache_k", [n_pool, 8192]); cache_v = din("cache_v", [n_pool, 8192]); cache_lf = din("cache_lf", [n_pool, 128])

    o_k = dout("o_k", [NT, 512]); o_v = dout("o_v", [NT, 512]); o_lf = dout("o_lf", [NT, 8])
    o_ks = dout("o_ks", [NS, 512]); o_vs = dout("o_vs", [NS, 512]); o_lfs = dout("o_lfs", [NS, 8]); o_gvs = dout("o_gvs", [NS, 512])
    o_xt = dout("o_xt", [D_MODEL, NTOK]); o_ht = dout("o_ht", [D_MODEL, NTOK], BF16)
    o_qt = dout("o_qt", [512, NT], BF16); o_kt = dout("o_kt", [512, NT], BF16); o_vb = dout("o_vb", [NT, 512], BF16)
    o_f = dout("o_f", [NT, 8]); o_mt = dout("o_mt", [512, NTOK], BF16)
    o_as = dout("o_as", [64, NS])

    outs = []
    with ExitStack() as st:
        P = Prog(nc, st)
        sb = lambda name, shape, dt: st.enter_context(nc.sbuf_tensor(name, shape, dt))
        psa = lambda name, shape, dt: st.enter_context(nc.psum_tensor(name, shape, dt))

        XTr = sb("XTr", [128, KC * NTOK * 2], BF16)
        XT = XTr[:, :].bitcast(F32).rearrange("p (k t) -> p k t", k=KC); t_xt = [T(f"XT{i}") for i in range(KC)]
        HTr = sb("HTr", [128, KC * NTOK], BF16)
        HT = HTr[:, :].rearrange("p (k t) -> p k t", k=KC); t_ht = [T(f"HT{i}") for i in range(KC)]
        HIDr = sb("HIDr", [128, 8 * NTOK], BF16)
        HID = HIDr[:, :].rearrange("p (k t) -> p k t", k=8); t_hid = [T(f"HID{i}") for i in range(8)]
        cst = sb("cst", [128, 384], F32); t_cst = T("cst")
        sm = sb("sm", [128, 2048], F32); t_sm = T("sm")
        ident = cst[:, 0:128]; tri_le = cst[:, 128:256]; gt = cst[:, 256:384]
        g1s = sm[:, 0:8]; gms = sm[:, 8:16]
        bf_rep = sm[:, 16:24]; qk_col = sm[:, 24:26]; bf_own = sm[:, 26:27]; ws00 = sm[:, 28:32]; bs0 = sm[:, 32:36]
        qg_rep = sm[:, 64:576]; kg_rep = sm[:, 576:1088]; gvg_rep = sm[:, 1088:1600]
        bs_rep = sm[:, 1600:2048 + 64] if False else None
        bsr = sb("bsr", [128, 4, 128], F32); t_bsr = T("bsr")
        ones_bf = sb("ones_bf", [128, 128], BF16); t_ones = T("ones")
        ones_f = sb("ones_f", [128, 128], F32); t_onesf = T("onesf")
        ident_bf = sb("ident_bf", [128, 128], BF16); t_idb = T("idb")
        eps_col = sb("eps_col", [128, 1], F32); t_eps = T("eps")
        sqr = sb("sqr", [128, KC * TW], BF16)
        sq = sqr[:, :].rearrange("p (k t) -> p k t", k=KC)
        rstd = sb("rstd", [128, TW], F32)
        sa = [sb(f"sa{i}", [128, TW], F32) for i in range(2)]
        C = dict(ones_bf=ones_bf, t_ones=t_ones, sq=sq, t_sq=[T() for _ in range(KC)], rstd=rstd, t_rstd=T(), eps_col=eps_col, t_eps=t_eps,
                 sa=sa, t_sa=[T(), T()])
        xin = Slots(P, sb, "xin", 2, [128, D_MODEL], F32)
        ws = WStream(P, sb, "w", [128, 8, 256], nf=2, nb=3)
        banks = [(psa(f"bk{i}", [128, 512], F32), T(f"bk{i}")) for i in range(7)]
        pbf = psa("pbf", [128, 1024], BF16); t_pbf = T("pbf")

        c_c = P.chan(); c_c2 = P.chan()
        P.add("sp", lambda e: e.dma_start(out=cst[:], in_=consts), writes=[t_cst], chan=c_c)
        P.add("sp", lambda e: e.dma_start(out=sm[:], in_=smalls), writes=[t_sm], chan=c_c2)
        P.add("dve", lambda e: e.memset(ones_bf[:], 1.0), writes=[t_ones])
        P.add("dve", lambda e: e.memset(ones_f[:], 1.0), writes=[t_onesf])
        P.add("dve", lambda e: e.memset(eps_col[:], EPS), writes=[t_eps])
        P.add("dve", lambda e: e.tensor_copy(out=ident_bf[:], in_=ident), reads=[t_cst], writes=[t_idb])

        for tt in range(17):
            rows = 128 if tt < 16 else NS
            src = x[tt * 128:(tt + 1) * 128, :] if tt < 16 else xs
            xt_, xtok, xch = xin.next()
            P.add("sp", (lambda e, xt_=xt_, src=src, rows=rows: e.dma_start(out=xt_[0:rows, :], in_=src)), writes=[xtok], chan=xch)
            for half in range(2):
                pt, t_pt = banks[(tt * 2 + half) % 4]
                for k4 in range(4):
                    kc = half * 4 + k4
                    P.add("pe", (lambda e, pt=pt, k4=k4, kc=kc, xt_=xt_, rows=rows: e.transpose(out=pt[:, k4 * 128:k4 * 128 + rows], in_=xt_[0:rows, kc * 128:(kc + 1) * 128],
                                                                                            identity=ident[0:rows, 0:rows])),
                          reads=[xtok, t_cst], writes=[t_pt])
                P.add("act", (lambda e, pt=pt, half=half, tt=tt, rows=rows: e.copy(out=XT[:, half * 4:half * 4 + 4, tt * 128:tt * 128 + rows],
                                                                                 in_=pt[:, :].rearrange("p (k t) -> p k t", k=4)[:, :, 0:rows])),
                      reads=[t_pt], writes=t_xt[half * 4:half * 4 + 4], wr_nosame=True)

        ws.extend(ffn_weight_plan(w_gu, w_down))
        emit_rmsnorm_fm(P, C, XT, t_xt, HT, t_ht, g1s, t_sm, banks[0:2], "n1")
        emit_ffn(P, C, ws, 0, XT, t_xt, HT, t_ht, HID, t_hid, w_gu, w_down, banks[0:6])

        emit_rmsnorm_fm(P, C, XT, t_xt, HT, t_ht, gms, t_sm, banks[0:2], "nm")
        c_xo = P.chan()
        outs.append(P.add("sp", lambda e: e.dma_start(out=o_xt.rearrange("(k p) t -> p k t", p=128), in_=XT), reads=t_xt, chan=c_xo))
        outs.append(P.add("sp", lambda e: e.dma_start(out=o_ht.rearrange("(k p) t -> p k t", p=128), in_=HT), reads=t_ht, chan=c_xo))
        if stage == "A":
            P.run(final_dma_ops=outs)
            return nc

        WINA = XTr[:, 0:KC * 2568].rearrange("p (k n) -> p k n", k=KC); t_wina = T("WINA")
        GV = XTr[:, KC * 2568:KC * 2568 + 17 * 512].rearrange("p (c n) -> p c n", c=17); t_gv = [T(f"GV{i}") for i in range(17)]
        UT = HID[:, 0:4, :]; t_ut = [T(f"UT{i}") for i in range(4)]
        wina_src = w_ina.rearrange("(kc p) n -> p kc n", p=128)
        for i in range(11):
            c0 = i * 256
            cw = 256 if i < 10 else 8
            ft, ftok, fch = ws.f.next()
            P.add("sp", (lambda e, ft=ft, c0=c0, cw=cw: e.dma_start(out=ft[:, :, 0:cw], in_=wina_src[:, :, c0:c0 + cw])), writes=[ftok], chan=fch)
            P.add("pool", (lambda e, ft=ft, c0=c0, cw=cw: e.tensor_copy(out=WINA[:, :, c0:c0 + cw], in_=ft[:, :, 0:cw])), reads=[ftok],
                  writes=[t_wina] + t_xt, wr_nosame=True)
        wsT_sb = sb("wsT_sb", [128, 8, 128], F32); t_wsT = T("wsT")
        wsTm = sb("wsTm", [128, 8, 128], BF16); t_wsTm = T("wsTm")
        c_ws = P.chan(); c_ws2 = P.chan()
        P.add("sp", lambda e: e.dma_start(out=wsT_sb[:], in_=wsT.rearrange("p (g t) -> p g t", g=8)), writes=[t_wsT], chan=c_ws)
        P.add("sp", lambda e: e.dma_start(out=bsr[:], in_=bsr_in.rearrange("p (f t) -> p f t", f=4)), writes=[t_bsr], chan=c_ws2)
        P.add("dve", lambda e: e.tensor_tensor(out=wsTm[:], in0=wsT_sb[:], in1=tri_le.unsqueeze(1).to_broadcast([128, 8, 128]), op=ALU.mult),
              reads=[t_wsT, t_cst], writes=[t_wsTm])

        hi = HIDr[:, 4 * NTOK:4 * NTOK + 8192].bitcast(F32)
        hv = [hi[:, i * 512:(i + 1) * 512] for i in range(8)]
        C["g1"] = [hv[0], hv[1]]; C["t_g1"] = [T(), T()]
        C["g2"] = [hv[2], hv[3]]; C["t_g2"] = [T(), T()]
        C["junk"] = [hv[4], hv[5]]; C["t_junk"] = [T(), T()]
        C["ss"] = [sb(f"ss{i}", [128, 8], F32) for i in range(2)]; C["t_ss"] = [T(), T()]
        kf = Slots(P, sb, "kf", 2, None, F32, tiles=[xin.tiles[0][:, 0:512], xin.tiles[0][:, 512:1024]])
        vf = Slots(P, sb, "vf", 2, None, F32, tiles=[xin.tiles[1][:, 0:512], xin.tiles[1][:, 512:1024]])
        gvf = Slots(P, sb, "gvf", 2, None, F32, tiles=[hv[7], sqr[:, 0:1024].bitcast(F32)])
        qf = Slots(P, sb, "qf", 1, None, F32, with_chan=False, tiles=[hv[6]])
        b16 = Slots(P, sb, "b16", 3, None, BF16, tiles=[sqr[:, 1024 + i * 512:1024 + (i + 1) * 512] for i in range(3)])
        tst = Slots(P, sb, "tst", 2, None, BF16, tiles=[sa[i][:, 0:256].bitcast(BF16).rearrange("p (j t) -> p j t", j=4) for i in range(2)])
        LF = rstd[:, 0:136].rearrange("p (t h) -> p t h", t=17); t_lf = [T(f"LF{i}") for i in range(17)]
        zt = rstd[:, 136:144]; t_zt = T("zt")
        c_lf = P.chan()
        pq, t_pq = banks[0]; pk, t_pk = banks[1]; pv, t_pv = banks[2]; pg, t_pg = banks[3]; pf, t_pf = banks[4]
        o_kt_v = o_kt.rearrange("(j p) t -> p j t", p=128)
        o_qt_v = o_qt.rearrange("(j p) t -> p j t", p=128)

        def transposed_store(src_f32, t_src, rows, tt, dst_view):
            bt, btok, _ = b16.next()
            P.add("act", (lambda e: e.copy(out=bt[0:rows, :], in_=src_f32)), reads=[t_src], writes=[btok])
            for j in range(4):
                P.add("pe", (lambda e, j=j: e.transpose(out=pbf[:, j * 128:j * 128 + rows], in_=bt[0:rows, j * 128:(j + 1) * 128], identity=ident_bf[0:rows, 0:rows])),
                      reads=[btok, t_idb], writes=[t_pbf])
            st_, sttok, stch = tst.next()
            P.add("dve", (lambda e: e.tensor_copy(out=st_[:, :, 0:rows], in_=pbf[:, 0:512].rearrange("p (j t) -> p j t", j=4)[:, :, 0:rows])),
                  reads=[t_pbf], writes=[sttok])
            outs.append(P.add("sp", (lambda e: e.dma_start(out=dst_view[:, :, tt * 128:tt * 128 + rows], in_=st_[:, :, 0:rows])), reads=[sttok], chan=stch))

        def tile_body(tt):
            rows = 128 if tt < 16 else NS
            tsl = slice(tt * 128, tt * 128 + rows)
            rsl = slice(tt * 128, tt * 128 + rows)
            for kc in range(KC):
                for (pp, tp, c0, cw) in ((pq, t_pq, 0, 512), (pk, t_pk, 512, 512), (pv, t_pv, 1024, 512), (pg, t_pg, 1536, 512), (pf, t_pf, 2560, 8)):
                    if tt == 16 and c0 == 0:
                        continue
                    P.add("pe", (lambda e, pp=pp, kc=kc, c0=c0, cw=cw: e.matmul(pp[0:rows, 0:cw], lhsT=HT[:, kc, tsl], rhs=WINA[:, kc, c0:c0 + cw],
                                                                               start=(kc == 0), stop=(kc == KC - 1))),
                          reads=[t_ht[kc], t_wina], writes=[tp])
            i2 = tt % 2
            kt_, ktok, kch = kf.next()
            emit_group_rmsnorm_tm(P, C, pk[0:rows, :], t_pk, rows, kg_rep, t_sm, kt_[0:rows, :], ktok, i2)
            dst = o_k[rsl, :] if tt < 16 else o_ks
            outs.append(P.add("sp", (lambda e, kt_=kt_, dst=dst: e.dma_start(out=dst, in_=kt_[0:rows, :])), reads=[ktok], chan=kch))
            if tt < 16:
                transposed_store(kt_[0:rows, :], ktok, rows, tt, o_kt_v)
            if tt < 16:
                qt_, qtok, _ = qf.next()
                emit_group_rmsnorm_tm(P, C, pq[0:rows, :], t_pq, rows, qg_rep, t_sm, qt_[0:rows, :], qtok, 1 - i2)
                transposed_store(qt_[0:rows, :], qtok, rows, tt, o_qt_v)
            vt_, vtok, vch = vf.next()
            P.add("act", (lambda e, vt_=vt_: e.copy(out=vt_[0:rows, :], in_=pv[0:rows, :])), reads=[t_pv], writes=[vtok])
            dst = o_v[rsl, :] if tt < 16 else o_vs
            outs.append(P.add("sp", (lambda e, vt_=vt_, dst=dst: e.dma_start(out=dst, in_=vt_[0:rows, :])), reads=[vtok], chan=vch))
            if tt < 16:
                bt, btok, bch = b16.next()
                P.add("act", (lambda e, bt=bt, vt_=vt_: e.copy(out=bt[0:rows, :], in_=vt_[0:rows, :])), reads=[vtok], writes=[btok])
                outs.append(P.add("sp", (lambda e, bt=bt: e.dma_start(out=o_vb[rsl, :], in_=bt[0:rows, :])), reads=[btok], chan=bch))
            P.add("dve", (lambda e: e.tensor_tensor(out=zt[0:rows, :], in0=pf[0:rows, 0:8], in1=bf_rep[0:rows, :], op=ALU.add)), reads=[t_pf, t_sm], writes=[t_zt])
            P.add("act", (lambda e: e.activation(out=zt[0:rows, :], in_=zt[0:rows, :], func=AF.Exp, scale=-1.0)), reads=[t_zt], writes=[t_zt])
            P.add("act", (lambda e: e.activation(out=zt[0:rows, :], in_=zt[0:rows, :], func=AF.Ln, bias=ones_f[0:rows, 0:1], scale=1.0)), reads=[t_zt, t_onesf], writes=[t_zt])
            P.add("dve", (lambda e, tt=tt: e.tensor_scalar(out=LF[0:rows, tt, :], in0=zt[0:rows, :], scalar1=-1.0, scalar2=None, op0=ALU.mult)),
                  reads=[t_zt], writes=[t_lf[tt]])
            dst = o_lf[rsl, :] if tt < 16 else o_lfs
            outs.append(P.add("sp", (lambda e, tt=tt, dst=dst: e.dma_start(out=dst, in_=LF[0:rows, tt, :])), reads=[t_lf[tt]], chan=c_lf))
            gtmp, t_gtmp = C["g2"][1 - i2], C["t_g2"][1 - i2]
            emit_gelu(P, C, gtmp[0:rows, :], [t_gtmp], pg[0:rows, :], t_pg, rows, 512, i2)
            gt_, gtok, gch = gvf.next()
            emit_group_rmsnorm_tm(P, C, gtmp[0:rows, :], t_gtmp, rows, gvg_rep, t_sm, gt_[0:rows, :], gtok, i2)
            if tt == 16:
                outs.append(P.add("sp", (lambda e, gt_=gt_: e.dma_start(out=o_gvs, in_=gt_[0:rows, :])), reads=[gtok], chan=gch))
            P.add("act", (lambda e, gt_=gt_, tt=tt: e.copy(out=GV[0:rows, tt, :], in_=gt_[0:rows, :])), reads=[gtok], writes=[t_gv[tt]])

        for tt in range(17):
            tile_body(tt)

        FS = rstd[:, 144:272].rearrange("p (t h) -> p t h", t=16); t_fs = T("FS")
        pF, t_pF = banks[4]
        for tt in range(16):
            P.add("pe", (lambda e, tt=tt: e.matmul(pF[:, tt * 8:(tt + 1) * 8], lhsT=tri_le, rhs=LF[:, tt, :], start=True, stop=(tt == 0))),
                  reads=[t_cst, t_lf[tt]], writes=[t_pF])
            for j in range(tt):
                P.add("pe", (lambda e, tt=tt, j=j: e.matmul(pF[:, tt * 8:(tt + 1) * 8], lhsT=ones_f[:], rhs=LF[:, j, :], start=False, stop=(j == tt - 1))),
                      reads=[t_onesf, t_lf[j]], writes=[t_pF])
        P.add("act", lambda e: e.copy(out=FS, in_=pF[:, 0:128].rearrange("p (t h) -> p t h", t=16)), reads=[t_pF], writes=[t_fs])
        c_fs = P.chan()
        outs.append(P.add("sp", lambda e: e.dma_start(out=o_f.rearrange("(t p) h -> p t h", p=128), in_=FS), reads=[t_fs], chan=c_fs))

        for m in range(4):
            for t in range(NTT):
                sl = slice(t * TW, (t + 1) * TW)
                pu, t_pu = banks[5 + (m * NTT + t) % 2]
                for kc in range(KC):
                    P.add("pe", (lambda e, kc=kc, m=m, sl=sl, pu=pu: e.matmul(pu[:, 0:TW], lhsT=WINA[:, kc, 2048 + m * 128:2048 + (m + 1) * 128], rhs=HT[:, kc, sl],
                                                                             start=(kc == 0), stop=(kc == KC - 1))),
                          reads=[t_wina, t_ht[kc]], writes=[t_pu])
                emit_gelu(P, C, UT[:, m, sl], [t_ut[m]], pu[:, 0:TW], t_pu, 128, TW, (m * NTT + t) % 2)

        gtmp2 = sb("gtmp2", [128, 2, 128], F32); t_gt2 = [T(), T()]
        for c in range(16):
            csl = slice(c * 128, (c + 1) * 128)
            for fc in range(4):
                par = (c * 4 + fc) % 2
                pG, t_pG = banks[5 + par]
                P.add("pe", (lambda e, c=c, fc=fc, pG=pG: e.matmul(pG[:, 0:128], lhsT=GV[:, c, fc * 128:(fc + 1) * 128], rhs=wsTm[:, 2 * fc, :], start=True, stop=True)),
                      reads=[t_gv[c], t_wsTm], writes=[t_pG])
                P.add("pe", (lambda e, c=c, fc=fc, pG=pG: e.matmul(pG[:, 128:256], lhsT=GV[:, c, fc * 128:(fc + 1) * 128], rhs=wsTm[:, 2 * fc + 1, :], start=True, stop=True)),
                      reads=[t_gv[c], t_wsTm], writes=[t_pG])
                P.add("dve", (lambda e, fc=fc, pG=pG, par=par: e.tensor_tensor(out=gtmp2[0:64, par, :], in0=pG[0:64, 0:128], in1=bsr[0:64, fc, :], op=ALU.add)),
                      reads=[t_pG, t_bsr], writes=[t_gt2[par]])
                P.add("dve", (lambda e, fc=fc, pG=pG, par=par: e.tensor_tensor(out=gtmp2[64:128, par, :], in0=pG[64:128, 128:256], in1=bsr[64:128, fc, :], op=ALU.add)),
                      reads=[t_pG, t_bsr], writes=[t_gt2[par]], wr_nosame=True)
                P.add("dve", (lambda e, fc=fc, csl=csl, par=par: e.tensor_tensor(out=UT[:, fc, csl], in0=gtmp2[:, par, :], in1=UT[:, fc, csl], op=ALU.mult)),
                      reads=[t_gt2[par]], writes=[t_ut[fc]], wr_nosame=True)
        for fc in range(4):
            P.add("pe", (lambda e, fc=fc: e.transpose(out=pbf[:, fc * 32:(fc + 1) * 32], in_=GV[0:NS, 16, fc * 128:(fc + 1) * 128], identity=ident_bf[0:NS, 0:NS])),
                  reads=[t_gv[16], t_idb], writes=[t_pbf])
        for fc in range(4):
            P.add("dve", (lambda e, fc=fc: e.tensor_scalar(out=gtmp2[:, 0, 0:NS], in0=pbf[:, fc * 32:(fc + 1) * 32], scalar1=ws00[:, fc:fc + 1], scalar2=bs0[:, fc:fc + 1],
                                                          op0=ALU.mult, op1=ALU.add)),
                  reads=[t_pbf, t_sm], writes=[t_gt2[0]])
            P.add("dve", (lambda e, fc=fc: e.tensor_tensor(out=UT[:, fc, NT:NTOK], in0=gtmp2[:, 0, 0:NS], in1=UT[:, fc, NT:NTOK], op=ALU.mult)),
                  reads=[t_gt2[0], t_ut[fc]], writes=[t_ut[fc]])
        c_mt = P.chan()
        outs.append(P.add("sp", lambda e: e.dma_start(out=o_mt.rearrange("(m p) t -> p m t", p=128), in_=UT), reads=t_ut, chan=c_mt))
        if stage == "B":
            P.run(final_dma_ops=outs)
            return nc

        MULENG = "dve"; REDENG = "dve"
        f0 = ws.f.tiles[0][:, :, :].rearrange("p a b -> p (a b)")
        lft = f0[:, 0:128]; lfT = f0[:, 128:256]; stile = f0[:, 256:384]; Pm = f0[:, 384:512]
        red = f0[:, 512:576]; qb = f0[:, 576:640]; qrep = f0[0:64, 640:768]
        Pmb = f0[:, 384:448].bitcast(BF16)
        orow = ws.f.tiles[1][:, :, :].rearrange("p a b -> p (a b)")[0:1, 0:2048]; t_orow = T("orow")
        eye32 = sm[0:1, 1600:2624] if False else None
        Tp = f0[:, 768:769]; bcol = f0[:, 769:770]; rs = f0[:, 770:771]
        sqs = f0[0:64, 784:880]; qkn = f0[0:64, 880:944]; sq2 = f0[0:64, 944:1008]; rq = f0[0:64, 1008:1072]
        frow = f0[0:1, 1072:1104]; FNB = f0[:, 1104:1136]; PN64 = f0[0:64, 1136:1168]; ASs = f0[0:64, 1168:1200]; DNs = f0[0:64, 1200:1232]
        prod = f0[0:64, 1232:1264]
        t_lft = T(); t_lfT = T(); t_st = T(); t_Pm = T(); t_red = T(); t_qb = T(); t_qrep = T(); t_Tp = T(); t_bcol = T(); t_rs = T()
        t_sqs = T(); t_qkn = T(); t_sq2 = T(); t_rq = T(); t_frow = T(); t_FNB = T(); t_PN = T(); t_ASs = T(); t_DNs = T(); t_prod = T()
        ptsb = sb("ptsb", [128, NS], I32); t_pts = T("pts")
        c_pt = P.chan(); c_e32 = P.chan()
        P.add("sp", lambda e: e.dma_start(out=ptsb[:], in_=ptT), writes=[t_pts], chan=c_pt)
        eye32r = f0[0:1, 1280:1280 + 1024] if False else sb("eye32r", [1, 1024], F32); t_e32 = T("e32")
        P.add("sp", lambda e: e.dma_start(out=eye32r[:], in_=eye32_in), writes=[t_e32], chan=c_e32)
        wown = ws.b.tiles[0]; t_wown = ws.b.toks[0]
        ft1, ftok1, fch1 = ws.f.tiles[1], ws.f.toks[1], ws.f.chans[1]
        P.add("sp", lambda e: e.dma_start(out=ft1[:, :, :], in_=w_own.rearrange("(kc p) n -> p kc n", p=128)), writes=[ftok1], chan=fch1)
        P.add("pool", lambda e: e.tensor_copy(out=wown[:, :, :], in_=ft1[:, :, :]), reads=[ftok1], writes=[t_wown])
        po, t_po = banks[5]
        for gi, (c0, m) in enumerate(((0, 64), (64, 64), (128, 64), (192, 1))):
            for kc in range(KC):
                P.add("pe", (lambda e, gi=gi, c0=c0, m=m, kc=kc: e.matmul(po[0:m, gi * 32:(gi + 1) * 32], lhsT=wown[:, kc, c0:c0 + m], rhs=HT[:, kc, NT:NTOK],
                                                                         start=(kc == 0), stop=(kc == KC - 1))),
                      reads=[t_wown, t_ht[kc]], writes=[t_po])
        P.add("act", lambda e: e.copy(out=sqs, in_=po[0:64, 0:96]), reads=[t_po], writes=[t_sqs])
        P.add("act", lambda e: e.copy(out=frow, in_=po[0:1, 96:128]), reads=[t_po], writes=[t_frow])
        P.add("dve", lambda e: e.tensor_tensor(out=sq2, in0=sqs[:, 0:64], in1=sqs[:, 0:64], op=ALU.mult), reads=[t_sqs], writes=[t_sq2])
        pS, t_pS = banks[6]
        P.add("pe", lambda e: e.matmul(pS[0:64, 0:64], lhsT=ones_f[0:64, 0:64], rhs=sq2, start=True, stop=True), reads=[t_sq2, t_onesf], writes=[t_pS])
        P.add("act", lambda e: e.activation(out=rq, in_=pS[0:64, 0:64], func=AF.Sqrt, bias=eps_col[0:64, :], scale=1.0 / 64), reads=[t_pS, t_eps], writes=[t_rq])
        P.add("dve", lambda e: e.reciprocal(out=rq, in_=rq), reads=[t_rq], writes=[t_rq])
        P.add("dve", lambda e: e.scalar_tensor_tensor(out=qkn[:, 0:32], in0=sqs[:, 0:32], scalar=qk_col[0:64, 0:1], in1=rq[:, 0:32], op0=ALU.mult, op1=ALU.mult),
              reads=[t_sqs, t_rq, t_sm], writes=[t_qkn])
        P.add("dve", lambda e: e.tensor_scalar(out=qkn[:, 0:32], in0=qkn[:, 0:32], scalar1=0.125, scalar2=None, op0=ALU.mult), reads=[t_qkn], writes=[t_qkn])
        P.add("dve", lambda e: e.scalar_tensor_tensor(out=qkn[:, 32:64], in0=sqs[:, 32:64], scalar=qk_col[0:64, 1:2], in1=rq[:, 32:64], op0=ALU.mult, op1=ALU.mult),
              reads=[t_sqs, t_rq, t_sm], writes=[t_qkn])
        P.add("dve", lambda e: e.tensor_tensor(out=prod, in0=qkn[:, 0:32], in1=qkn[:, 32:64], op=ALU.mult), reads=[t_qkn], writes=[t_prod])
        P.add("pe", lambda e: e.matmul(pS[0:64, 64:96], lhsT=ones_f[0:64, 0:64], rhs=prod, start=True, stop=True), reads=[t_prod, t_onesf], writes=[t_pS])
        P.add("act", lambda e: e.activation(out=PN64, in_=pS[0:64, 64:96], func=AF.Exp), reads=[t_pS], writes=[t_PN])
        P.add("dve", lambda e: e.tensor_scalar(out=frow, in0=frow, scalar1=bf_own[0:1, :], scalar2=None, op0=ALU.add), reads=[t_frow, t_sm], writes=[t_frow])
        P.add("act", lambda e: e.activation(out=frow, in_=frow, func=AF.Exp, scale=-1.0), reads=[t_frow], writes=[t_frow])
        P.add("act", lambda e: e.activation(out=frow, in_=frow, func=AF.Ln, bias=ones_f[0:1, 0:1], scale=1.0), reads=[t_frow, t_onesf], writes=[t_frow])
        P.add("dve", lambda e: e.tensor_scalar(out=frow, in0=frow, scalar1=-1.0, scalar2=None, op0=ALU.mult), reads=[t_frow], writes=[t_frow])
        P.add("pe", lambda e: e.matmul(pS[:, 96:128], lhsT=ones_f[0:1, 0:128], rhs=frow, start=True, stop=True), reads=[t_frow, t_onesf], writes=[t_pS])
        P.add("act", lambda e: e.copy(out=FNB, in_=pS[:, 96:128]), reads=[t_pS], writes=[t_FNB])

        Kslots = [XTr[:, 0:16384].bitcast(F32)]; Vslots = [XTr[:, 16384:32768].bitcast(F32)]
        t_Ks = [T("K0"), T("K1")]; t_Vs = [T("V0"), T("V1")]
        c_ks = [P.chan(), P.chan()]; c_vs = [P.chan(), P.chan()]; c_l = P.chan()
        dead = [t_wina] + t_gv + t_xt
        ck4 = cache_k.rearrange("n (a c) -> (n a) c", a=4); cv4 = cache_v.rearrange("n (a c) -> (n a) c", a=4)
        ptx = sb("ptx", [128, 4, NS], I32); t_ptx = T("ptx")
        for a in range(4):
            P.add("dve", (lambda e, a=a: e.tensor_scalar(out=ptx[:, a, :], in0=ptsb[:], scalar1=4, scalar2=a, op0=ALU.mult, op1=ALU.add)),
                  reads=[t_pts], writes=[t_ptx], wr_nosame=True)
        pQ = pS[:, 128:192]; pT = pS[:, 192:320]; pW = pS[:, 320:448]; pSp = pS[:, 448:449]
        pAS = pS[0:64, 450:482]; pDN = pS[0:64, 482:514] if False else None
        pB, t_pB = banks[5]
        pAS = pB[0:64, 128:160]; pDN = pB[0:64, 160:192]
        t_pQ = T(); t_pT = T(); t_pW = T(); t_pSp = T(); t_pAS = T(); t_pDN = T()

        def sample_body(b):
            sb_ = 0
            Ks, Vs, t_K, t_V, c_k, c_v = Kslots[sb_], Vslots[sb_], t_Ks[sb_], t_Vs[sb_], c_ks[sb_], c_vs[sb_]
            P.add("pool", (lambda e: e.indirect_dma_start(out=Ks, out_offset=None, in_=cache_k,
                                                          in_offset=bass.IndirectOffsetOnAxis(ap=ptsb[:, b:b + 1], axis=0))),
                  reads=[t_pts], writes=[t_K] + (dead if b < 1 else []), chan=c_k)
            P.add("pool", (lambda e: e.indirect_dma_start(out=Vs, out_offset=None, in_=cache_v,
                                                          in_offset=bass.IndirectOffsetOnAxis(ap=ptsb[:, b:b + 1], axis=0))),
                  reads=[t_pts], writes=[t_V] + (dead if b < 1 else []), chan=c_v)
            P.add("pool", (lambda e: e.indirect_dma_start(out=lft, out_offset=None, in_=cache_lf,
                                                          in_offset=bass.IndirectOffsetOnAxis(ap=ptsb[:, b:b + 1], axis=0))),
                  reads=[t_pts], writes=[t_lft], chan=c_l)
            P.add("dve", (lambda e: e.tensor_copy(out=qrep, in_=qkn[:, b:b + 1].to_broadcast([64, 128]))), reads=[t_qkn], writes=[t_qrep])
            P.add("pe", (lambda e: e.matmul(pQ, lhsT=qrep, rhs=ident[0:64, 0:64], start=True, stop=True)), reads=[t_qrep, t_cst], writes=[t_pQ])
            P.add("act", (lambda e: e.copy(out=qb, in_=pQ)), reads=[t_pQ], writes=[t_qb])
            K3 = Ks.rearrange("p (s d) -> p s d", d=64)
            P.add(MULENG, (lambda e: e.tensor_tensor(out=K3, in0=K3, in1=qb.unsqueeze(1).to_broadcast([128, 128, 64]), op=ALU.mult)), reads=[t_K, t_qb], writes=[t_K])
            P.add(REDENG, (lambda e: e.tensor_reduce(out=stile, in_=K3, axis=AX.X, op=ALU.add)), reads=[t_K], writes=[t_st])
            P.add("pe", (lambda e: e.transpose(out=pT, in_=lft, identity=ident)), reads=[t_lft, t_cst], writes=[t_pT])
            P.add("act", (lambda e: e.copy(out=lfT, in_=pT)), reads=[t_pT], writes=[t_lfT])
            P.add("pe", (lambda e: e.matmul(pW, lhsT=lfT, rhs=gt, start=True, stop=True)), reads=[t_lfT, t_cst], writes=[t_pW])
            P.add("dve", (lambda e: e.tensor_reduce(out=Tp, in_=lft, axis=AX.X, op=ALU.add)), reads=[t_lft], writes=[t_Tp])
            P.add("pe", (lambda e: e.matmul(pSp, lhsT=gt, rhs=Tp, start=True, stop=True)), reads=[t_Tp, t_cst], writes=[t_pSp])
            P.add("dve", (lambda e: e.tensor_tensor(out=bcol, in0=pSp, in1=FNB[:, b:b + 1], op=ALU.add)), reads=[t_pSp, t_FNB], writes=[t_bcol])
            P.add("dve", (lambda e: e.tensor_tensor(out=stile, in0=stile, in1=pW, op=ALU.add)), reads=[t_st, t_pW], writes=[t_st])
            P.add("act", (lambda e: e.activation(out=Pm, in_=stile, func=AF.Exp, bias=bcol, scale=1.0, accum_out=rs)), reads=[t_st, t_bcol], writes=[t_Pm, t_rs])
            V3 = Vs.rearrange("p (s d) -> p s d", d=64)
            P.add(MULENG, (lambda e: e.tensor_tensor(out=V3, in0=V3, in1=Pm.unsqueeze(2).to_broadcast([128, 128, 64]), op=ALU.mult)), reads=[t_V, t_Pm], writes=[t_V])
            P.add(REDENG, (lambda e: e.tensor_reduce(out=red, in_=Vs.rearrange("p (s d) -> p d s", d=64), axis=AX.X, op=ALU.add)), reads=[t_V], writes=[t_red])
            P.add("pe", (lambda e: e.matmul(pAS[:, b:b + 1], lhsT=red, rhs=ones_f[:, 0:1], start=True, stop=True)), reads=[t_red, t_onesf], writes=[t_pAS])
            P.add("pe", (lambda e: e.matmul(pDN[:, b:b + 1], lhsT=ones_f[:, 0:64], rhs=rs, start=True, stop=True)), reads=[t_rs, t_onesf], writes=[t_pDN])

        for b in range(NS):
            sample_body(b)
        P.add("dve", lambda e: e.tensor_tensor(out=ASs, in0=PN64, in1=sqs[:, 64:96], op=ALU.mult), reads=[t_PN, t_sqs], writes=[t_ASs])
        P.add("dve", lambda e: e.tensor_tensor(out=ASs, in0=ASs, in1=pAS, op=ALU.add), reads=[t_ASs, t_pAS], writes=[t_ASs])
        P.add("dve", lambda e: e.tensor_tensor(out=DNs, in0=PN64, in1=pDN, op=ALU.add), reads=[t_PN, t_pDN], writes=[t_DNs])
        P.add("dve", lambda e: e.reciprocal(out=DNs, in_=DNs), reads=[t_DNs], writes=[t_DNs])
        P.add("dve", lambda e: e.tensor_tensor(out=ASs, in0=ASs, in1=DNs, op=ALU.mult), reads=[t_ASs, t_DNs], writes=[t_ASs])
        c_as = P.chan()
        outs.append(P.add("sp", lambda e: e.dma_start(out=o_as, in_=ASs), reads=[t_ASs], chan=c_as))
        P.run(final_dma_ops=outs)
    return nc


def build_l2(stage="full"):
    nc = bass.Bass("TRN2", target_bir_lowering=False)
    din = lambda name, shape, dt=F32: nc.dram_tensor(name, shape, dt, kind="ExternalInput").ap()
    dout = lambda name, shape, dt=F32: nc.dram_tensor(name, shape, dt, kind="ExternalOutput").ap()
    i_xt = din("i_xt", [D_MODEL, NTOK]); i_ht = din("i_ht", [D_MODEL, NTOK], BF16)
    i_qt = din("i_qt", [512, NT], BF16); i_kt = din("i_kt", [512, 2 * NT], BF16); i_vb = din("i_vb", [2 * NT, 512], BF16)
    i_f = din("i_f", [2 * NT, 8]); i_mt = din("i_mt", [512, NTOK], BF16); i_as = din("i_as", [512, NS])
    w_ing = din("w_ing", [D_MODEL, 2048]); w_pa = din("w_pa", [512, D_MODEL]); w_pg = din("w_pg", [512, D_MODEL]); w_o = din("w_o", [D_MODEL, D_MODEL])
    w_gu = din("w_gu", [D_MODEL, 2 * D_FF]); w_down = din("w_down", [D_FF, D_MODEL])
    w_plg = din("w_plg", [D_MODEL, D_MODEL]); w_plp = din("w_plp", [256, D_MODEL])
    pin = din("pin", [NT, 256]); pins = din("pins", [NS, 256])
    smalls = din("smalls", [128, 64]); consts = din("consts", [128, 384])
    o_y = dout("o_y", [NT, D_MODEL]); o_ys = dout("o_ys", [NS, D_MODEL])
    o_at = dout("o_at", [512, NTOK], BF16) if stage in ("A", "M") else None
    o_dbb = dout("o_dbb", [D_MODEL, NTOK], BF16) if stage in ("M",) else None
    o_dbf = dout("o_dbf", [D_MODEL, NTOK], F32) if stage in ("X2", "X3") else None

    outs = []
    with ExitStack() as st:
        P = Prog(nc, st)
        sb = lambda name, shape, dt: st.enter_context(nc.sbuf_tensor(name, shape, dt))
        psa = lambda name, shape, dt: st.enter_context(nc.psum_tensor(name, shape, dt))
        XTr = sb("XTr", [128, KC * NTOK * 2], BF16)
        XT = XTr[:, :].bitcast(F32).rearrange("p (k t) -> p k t", k=KC); t_xt = [T(f"XT{i}") for i in range(KC)]
        HTr = sb("HTr", [128, KC * NTOK], BF16)
        HT = HTr[:, :].rearrange("p (k t) -> p k t", k=KC); t_ht = [T(f"HT{i}") for i in range(KC)]
        HIDr = sb("HIDr", [128, 8 * NTOK], BF16)
        HID = HIDr[:, :].rearrange("p (k t) -> p k t", k=8); t_hid = [T(f"HID{i}") for i in range(8)]
        AT = sb("AT", [128, 4, NTOK], BF16); t_at = [T(f"AT{i}") for i in range(4)]
        MTr = sb("MTr", [128, 4 * NTOK], BF16)
        MT = MTr[:, :].rearrange("p (k t) -> p k t", k=4); t_mt = T("MT")
        cst = sb("cst", [128, 384], F32); t_cst = T("cst")
        sm = sb("sm", [128, 64], F32); t_sm = T("sm")
        ident = cst[:, 0:128]; tri_le = cst[:, 128:256]
        g2s = sm[:, 0:8]; gps = sm[:, 8:16]; pbias = sm[:, 16:17]
        ones_bf = sb("ones_bf", [128, 128], BF16); t_ones = T("ones")
        tri_bf = sb("tri_bf", [128, 128], BF16); t_tri = T("tri")
        eps_col = sb("eps_col", [128, 1], F32); t_eps = T("eps")
        sqr = sb("sqr", [128, KC * TW], BF16)
        sq = sqr[:, :].rearrange("p (k t) -> p k t", k=KC)
        rstd = sb("rstd", [128, TW], F32)
        sa = [sb(f"sa{i}", [128, TW], F32) for i in range(2)]
        C = dict(ones_bf=ones_bf, t_ones=t_ones, sq=sq, t_sq=[T() for _ in range(KC)], rstd=rstd, t_rstd=T(), eps_col=eps_col, t_eps=t_eps,
                 sa=sa, t_sa=[T(), T()])
        ws = WStream(P, sb, "w", [128, 8, 256], nf=2, nb=3, depth=1)
        banks = [(psa(f"bk{i}", [128, 512], F32), T(f"bk{i}")) for i in range(8)]

        c_c = P.chan(); c_c2 = P.chan()
        P.add("sp", lambda e: e.dma_start(out=cst[:], in_=consts), writes=[t_cst], chan=c_c)
        P.add("sp", lambda e: e.dma_start(out=sm[:], in_=smalls), writes=[t_sm], chan=c_c2)
        P.add("dve", lambda e: e.memset(ones_bf[:], 1.0), writes=[t_ones])
        P.add("dve", lambda e: e.memset(eps_col[:], EPS), writes=[t_eps])
        P.add("dve", lambda e: e.tensor_copy(out=tri_bf[:], in_=tri_le), reads=[t_cst], writes=[t_tri])
        c_in = P.chan(); c_in2 = P.chan()
        P.add("sp", lambda e: e.dma_start(out=HT, in_=i_ht.rearrange("(k p) t -> p k t", p=128)), writes=t_ht, chan=c_in)
        P.add("sp", lambda e: e.dma_start(out=MT, in_=i_mt.rearrange("(k p) t -> p k t", p=128)), writes=[t_mt], chan=c_in2)
        c_as = P.chan()
        P.add("pool", lambda e: e.dma_start(out=AT[:, :, NT:NTOK], in_=i_as.rearrange("(k p) t -> p k t", p=128)), writes=t_at, chan=c_as)

        xr = XTr
        off = [0]

        def carve(nel, dt=BF16):
            a = xr[:, off[0]:off[0] + nel]
            off[0] += nel
            return a if dt == BF16 else a.bitcast(dt)
        KTs = [carve(4096) for _ in range(2)]; t_kts = [T(), T()]
        QTs = [carve(2048) for _ in range(2)]; t_qts = [T(), T()]
        Vld = [carve(32 * 128).rearrange("p (k n) -> p k n", k=32) for _ in range(1)]; t_vld = [T()]
        VA = [carve(32 * 128).rearrange("p (k n) -> p k n", k=32) for _ in range(2)]; t_va = [T(), T()]
        Pt = [carve(512) for _ in range(4)]; t_ptl = [T(), T(), T(), T()]
        Fk = carve(32 * 8 * 2, F32).rearrange("p (k h) -> p k h", k=32); t_fk = T()
        Fref = carve(5 * 8 * 2, F32).rearrange("p (q h) -> p q h", q=5); t_fref = T()
        Bm = carve(32 * 4 * 8 * 2, F32).rearrange("p (k q h) -> p k q h", k=32, q=4); t_bm = T()
        rden = carve(512 * 2, F32); t_rden = T()
        c_k = [P.chan(), P.chan()]; c_q = [P.chan(), P.chan()]; c_v = P.chan(); c_f = P.chan()
        P.add("sp", lambda e: e.dma_start(out=Fk, in_=i_f.rearrange("(k p) h -> p k h", p=128)), writes=[t_fk], chan=c_f)
        for qt in range(4):
            r = NT + qt * 512
            P.add("sp", (lambda e, qt=qt, r=r: e.dma_start(out=Fref[:, qt, :], in_=i_f[r:r + 1, :].partition_broadcast(128))), writes=[t_fref], chan=c_f, wr_nosame=True)
        P.add("sp", lambda e: e.dma_start(out=Fref[:, 4, :], in_=i_f[NT - 1:NT, :].partition_broadcast(128)), writes=[t_fref], chan=c_f, wr_nosame=True)
        P.add("pool", lambda e: e.memset(VA[0][:, :, 0:64], 1.0), writes=[t_va[0]])
        P.add("pool", lambda e: e.memset(VA[1][:, :, 64:128], 1.0), writes=[t_va[1]])
        P.add("dve", lambda e: e.tensor_tensor(out=Fk[:, 0:16, :], in0=Fk[:, 0:16, :], in1=Fref[:, 4:5, :].to_broadcast([128, 16, 8]), op=ALU.subtract),
              reads=[t_fk, t_fref], writes=[t_fk])
        P.add("dve", lambda e: e.tensor_scalar(out=Fk[:, 0:16, :], in0=Fk[:, 0:16, :], scalar1=pbias, scalar2=None, op0=ALU.subtract), reads=[t_fk, t_sm], writes=[t_fk])
        for qt in range(4):
            P.add("dve", (lambda e, qt=qt: e.tensor_tensor(out=Bm[:, :, qt, :], in0=Fref[:, qt:qt + 1, :].to_broadcast([128, 32, 8]), in1=Fk, op=ALU.subtract)),
                  reads=[t_fk, t_fref], writes=[t_bm], wr_nosame=True)

        o_at_v = o_at.rearrange("(k p) t -> p k t", p=128) if o_at is not None else None

        def attn_chunk(c):
            par = c % 2
            kt_, qt_ = KTs[par], QTs[par]
            P.add("sp", (lambda e: e.dma_start(out=kt_, in_=i_kt[c * 128:(c + 1) * 128, :])), writes=[t_kts[par]], chan=c_k[par])
            P.add("sp", (lambda e: e.dma_start(out=qt_, in_=i_qt[c * 128:(c + 1) * 128, :])), writes=[t_qts[par]], chan=c_q[par])
            P.add("sp", (lambda e: e.dma_start(out=Vld[0], in_=i_vb[:, c * 128:(c + 1) * 128].rearrange("(k p) n -> p k n", p=128))), writes=[t_vld[0]], chan=c_v)
            for hl in range(2):
                P.add("dve", (lambda e, hl=hl: e.tensor_copy(out=VA[hl][:, :, (1 - hl) * 64:(2 - hl) * 64], in_=Vld[0][:, :, hl * 64:(hl + 1) * 64])), reads=[t_vld[0]], writes=[t_va[hl]], wr_nosame=True)
            for hl in range(2):
                h = 2 * c + hl
                prt = slice(hl * 64, (hl + 1) * 64)
                for qt in range(4):
                    pA, t_pA = banks[4 + (h * 4 + qt) % 2]
                    kts = list(range(16)) + [16 + ko for ko in range(4 * qt + 4)]
                    n = len(kts)
                    LA = 2
                    info = {}
                    for i in range(n + LA):
                        if i < n:
                            kt = kts[i]
                            ko = kt - 16
                            j0 = 0
                            if ko >= 0 and ko * 128 > qt * 512:
                                j0 = ko * 128 - qt * 512
                            diag = ko >= 0 and ko * 128 >= qt * 512
                            w = 512 - j0
                            q0 = qt * 512 + j0
                            pS, t_pS = banks[i % 4]
                            pt_, t_pt = Pt[i % 4], t_ptl[i % 4]
                            P.add("pe", (lambda e, pS=pS, kt=kt, q0=q0, w=w, prt=prt: e.matmul(pS[:, 0:w], lhsT=kt_[prt, kt * 128:(kt + 1) * 128], rhs=qt_[prt, q0:q0 + w], start=True, stop=True)),
                                  reads=[t_kts[par], t_qts[par]], writes=[t_pS])
                            P.add("act", (lambda e, pS=pS, pt_=pt_, kt=kt, qt=qt, h=h, w=w: e.activation(out=pt_[:, 0:w], in_=pS[:, 0:w], func=AF.Exp, bias=Bm[:, kt, qt, h:h + 1], scale=0.125)),
                                  reads=[t_pS, t_bm], writes=[t_pt])
                            if diag:
                                P.add("dve", (lambda e, pt_=pt_: e.tensor_tensor(out=pt_[:, 0:128], in0=pt_[:, 0:128], in1=tri_le, op=ALU.mult)), reads=[t_pt, t_cst], writes=[t_pt])
                            info[i] = (pt_, t_pt, kt, j0, w)
                        j = i - LA
                        if j >= 0:
                            ppt, ptok, pkt, pj0, pw = info.pop(j)
                            P.add("pe", (lambda e, pA=pA, ppt=ppt, pkt=pkt, pj0=pj0, pw=pw, j=j, n=n, hl=hl: e.matmul(pA[:, pj0:512], lhsT=VA[hl][:, pkt, :], rhs=ppt[:, 0:pw],
                                                                                                               start=(j == 0), stop=(j == n - 1))),
                                  reads=[t_va[hl], ptok], writes=[t_pA])
                    dsl = slice(hl * 64, (hl + 1) * 64)
                    nsl = slice((1 - hl) * 64, (2 - hl) * 64)
                    P.add("dve", (lambda e, pA=pA, dsl=dsl: e.reciprocal(out=rden[dsl, :], in_=pA[dsl, :])), reads=[t_pA], writes=[t_rden])
                    P.add("dve", (lambda e, pA=pA, qt=qt, dsl=dsl, nsl=nsl: e.tensor_tensor(out=AT[dsl, c, qt * 512:(qt + 1) * 512], in0=pA[nsl, :], in1=rden[dsl, :], op=ALU.mult)),
                          reads=[t_pA, t_rden], writes=[t_at[c]], wr_nosame=True)

        for c in range(4):
            attn_chunk(c)
        c_ao = P.chan()
        if stage == "A":
            outs.append(P.add("sp", lambda e: e.dma_start(out=o_at_v, in_=AT[:]), reads=t_at, chan=c_ao))
            P.run(final_dma_ops=outs)
            return nc

        attn_toks = t_kts + t_qts + t_vld + t_va + t_ptl + [t_fk, t_fref, t_bm, t_rden]
        sa2 = [sqr[:, i * 832:(i + 1) * 832].bitcast(F32) for i in range(2)]; t_sa2 = [T(), T()]
        plan = []
        ging = w_ing.rearrange("(kc p) n -> p kc n", p=128)
        gpa = w_pa.rearrange("(kc p) n -> p kc n", p=128); gpg = w_pg.rearrange("(kc p) n -> p kc n", p=128)
        gwo = w_o.rearrange("(kc p) n -> p kc n", p=128)
        for m in range(8):
            plan.append([((lambda tl: tl[:, :, 0:128]), ging[:, :, m * 128:(m + 1) * 128]),
                         ((lambda tl: tl[:, :, 128:256]), ging[:, :, 1024 + m * 128:1024 + (m + 1) * 128])])
            plan.append([((lambda tl: tl[:, 0:4, 0:128]), gpa[:, :, m * 128:(m + 1) * 128]),
                         ((lambda tl: tl[:, 4:8, 0:128]), gpg[:, :, m * 128:(m + 1) * 128])])
        for n in range(8):
            plan.append([((lambda tl: tl[:, :, 0:128]), gwo[:, :, n * 128:(n + 1) * 128])])
        ws.extend(plan)
        ffn_base = len(ws.plan)
        ws.extend(ffn_weight_plan(w_gu, w_down))
        ple_base = len(ws.plan)
        gplg = w_plg.rearrange("(kc p) n -> p kc n", p=128); gplp = w_plp.rearrange("(kc p) n -> p kc n", p=128)
        plan = []
        for n in range(8):
            plan.append([((lambda tl: tl[:, :, 0:128]), gplg[:, :, n * 128:(n + 1) * 128]),
                         ((lambda tl: tl[:, 0:2, 128:256]), gplp[:, :, n * 128:(n + 1) * 128])])
        ws.extend(plan)

        def merge_m(m):
            wg, wgtok = ws.get(2 * m)
            wp, wptok = ws.get(2 * m + 1)
            for t in range(NTT):
                sl = slice(t * TW, (t + 1) * TW)
                par = (m * NTT + t) % 2
                (ga, t_ga), (gb, t_gb), (pa, t_pa), (pb, t_pb) = banks[4 * par:4 * par + 4]
                for kc in range(KC):
                    P.add("pe", (lambda e, kc=kc, sl=sl, ga=ga: e.matmul(ga[:, 0:TW], lhsT=wg[:, kc, 0:128], rhs=HT[:, kc, sl], start=(kc == 0), stop=(kc == KC - 1))),
                          reads=[wgtok, t_ht[kc]], writes=[t_ga])
                for kc in range(KC):
                    P.add("pe", (lambda e, kc=kc, sl=sl, gb=gb: e.matmul(gb[:, 0:TW], lhsT=wg[:, kc, 128:256], rhs=HT[:, kc, sl], start=(kc == 0), stop=(kc == KC - 1))),
                          reads=[wgtok, t_ht[kc]], writes=[t_gb])
                for k4 in range(4):
                    P.add("pe", (lambda e, k4=k4, sl=sl, pa=pa: e.matmul(pa[:, 0:TW], lhsT=wp[:, k4, 0:128], rhs=AT[:, k4, sl], start=(k4 == 0), stop=(k4 == 3))),
                          reads=[wptok, t_at[k4]], writes=[t_pa])
                for k4 in range(4):
                    P.add("pe", (lambda e, k4=k4, sl=sl, pb=pb: e.matmul(pb[:, 0:TW], lhsT=wp[:, 4 + k4, 0:128], rhs=MT[:, k4, sl], start=(k4 == 0), stop=(k4 == 3))),
                          reads=[wptok, t_mt], writes=[t_pb])
                sA, t_sA = (sa[par], C["t_sa"][par])
                sB, t_sB = (sa2[par], t_sa2[par])
                P.add("act", (lambda e, ga=ga, sA=sA: e.activation(out=sA[:], in_=ga[:, 0:TW], func=AF.Sigmoid)), reads=[t_ga], writes=[t_sA])
                P.add("act", (lambda e, gb=gb, sB=sB: e.activation(out=sB[:], in_=gb[:, 0:TW], func=AF.Sigmoid)), reads=[t_gb], writes=[t_sB] + C["t_sq"], wr_nosame=True)
                P.add("dve", (lambda e, pa=pa, sA=sA: e.tensor_tensor(out=sA[:], in0=sA[:], in1=pa[:, 0:TW], op=ALU.mult)), reads=[t_sA, t_pa], writes=[t_sA])
                P.add("dve", (lambda e, pb=pb, sB=sB: e.tensor_tensor(out=sB[:], in0=sB[:], in1=pb[:, 0:TW], op=ALU.mult)), reads=[t_sB, t_pb], writes=[t_sB])
                P.add("dve", (lambda e, sA=sA, sB=sB, sl=sl: e.tensor_tensor(out=HID[:, m, sl], in0=sA[:], in1=sB[:], op=ALU.add)), reads=[t_sA, t_sB], writes=[t_hid[m]], wr_nosame=True)

        for m in range(8):
            merge_m(m)

        if stage == "M":
            outs.append(P.add("sp", lambda e: e.dma_start(out=o_dbb.rearrange("(k p) t -> p k t", p=128), in_=HID), reads=t_hid, chan=c_ao))
            outs.append(P.add("sp", lambda e: e.dma_start(out=o_at_v, in_=AT[:]), reads=t_at, chan=c_ao))
            P.run(final_dma_ops=outs)
            return nc
        c_x = [P.chan() for _ in range(8)]

        def wout_n(n):
            wo, wotok = ws.get(16 + n)
            P.add("sp", (lambda e: e.dma_start(out=XT[:, n, :], in_=i_xt[n * 128:(n + 1) * 128, :])), writes=[t_xt[n]] + attn_toks, chan=c_x[n], wr_nosame=True)
            for t in range(NTT):
                sl = slice(t * TW, (t + 1) * TW)
                po, t_po = banks[(n * NTT + t) % 2]
                for m in range(8):
                    P.add("pe", (lambda e, m=m, sl=sl, po=po: e.matmul(po[:, 0:TW], lhsT=wo[:, m, 0:128], rhs=HID[:, m, sl], start=(m == 0), stop=(m == 7))),
                          reads=[wotok, t_hid[m]], writes=[t_po])
                P.add("dve", (lambda e, sl=sl, po=po: e.tensor_tensor(out=XT[:, n, sl], in0=XT[:, n, sl], in1=po[:, 0:TW], op=ALU.add)),
                      reads=[t_po, t_xt[n]], writes=[t_xt[n]], wr_nosame=True)

        for n in range(8):
            wout_n(n)

        if stage == "X2":
            outs.append(P.add("sp", lambda e: e.dma_start(out=o_dbf.rearrange("(k p) t -> p k t", p=128), in_=XT), reads=t_xt, chan=c_ao))
            P.run(final_dma_ops=outs)
            return nc
        emit_rmsnorm_fm(P, C, XT, t_xt, HT, t_ht, g2s, t_sm, banks[0:2], "n2")
        emit_ffn(P, C, ws, ffn_base, XT, t_xt, HT, t_ht, HID, t_hid, w_gu, w_down, banks[0:6])
        if stage == "X3":
            outs.append(P.add("sp", lambda e: e.dma_start(out=o_dbf.rearrange("(k p) t -> p k t", p=128), in_=XT), reads=t_xt, chan=c_ao))
            P.run(final_dma_ops=outs)
            return nc

        emit_rmsnorm_fm(P, C, XT, t_xt, HT, t_ht, gps, t_sm, banks[0:2], "np")
        PT = AT[:, 0:2, :]
        pslots = Slots(P, sb, "pin", 2, None, F32, tiles=[MTr[:, i * 512:(i + 1) * 512].bitcast(F32) for i in range(2)])
        for tt in range(17):
            rows = 128 if tt < 16 else NS
            src = pin[tt * 128:(tt + 1) * 128, :] if tt < 16 else pins
            pt_, ptok, pch = pslots.next()
            P.add("sp", (lambda e, pt_=pt_, src=src, rows=rows: e.dma_start(out=pt_[0:rows, :], in_=src)), writes=[ptok, t_mt], chan=pch, wr_nosame=True)
            pp, t_pp = banks[6 + tt % 2]
            for k2 in range(2):
                P.add("pe", (lambda e, pp=pp, k2=k2, pt_=pt_, rows=rows: e.transpose(out=pp[:, k2 * 128:k2 * 128 + rows], in_=pt_[0:rows, k2 * 128:(k2 + 1) * 128], identity=ident[0:rows, 0:rows])),
                      reads=[ptok, t_cst], writes=[t_pp])
            P.add("act", (lambda e, pp=pp, tt=tt, rows=rows: e.copy(out=PT[:, :, tt * 128:tt * 128 + rows], in_=pp[:, 0:256].rearrange("p (k t) -> p k t", k=2)[:, :, 0:rows])),
                  reads=[t_pp], writes=[t_at[0], t_at[1]], wr_nosame=True)

        def ple_n(n):
            wl, wltok = ws.get(ple_base + n)
            for t in range(NTT):
                sl = slice(t * TW, (t + 1) * TW)
                par = (n * NTT + t) % 2
                pg_, t_pg_ = banks[2 * par]
                pw_, t_pw_ = banks[2 * par + 1]
                for kc in range(KC):
                    P.add("pe", (lambda e, kc=kc, sl=sl, pg_=pg_: e.matmul(pg_[:, 0:TW], lhsT=wl[:, kc, 0:128], rhs=HT[:, kc, sl], start=(kc == 0), stop=(kc == KC - 1))),
                          reads=[wltok, t_ht[kc]], writes=[t_pg_])
                for k2 in range(2):
                    P.add("pe", (lambda e, k2=k2, sl=sl, pw_=pw_: e.matmul(pw_[:, 0:TW], lhsT=wl[:, k2, 128:256], rhs=PT[:, k2, sl], start=(k2 == 0), stop=(k2 == 1))),
                          reads=[wltok, t_at[k2]], writes=[t_pw_])
                sA, t_sA = (sa[par], C["t_sa"][par])
                P.add("act", (lambda e, pg_=pg_, sA=sA: e.activation(out=sA[:], in_=pg_[:, 0:TW], func=AF.Sigmoid)), reads=[t_pg_], writes=[t_sA])
                P.add("dve", (lambda e, pw_=pw_, sA=sA: e.tensor_tensor(out=sA[:], in0=sA[:], in1=pw_[:, 0:TW], op=ALU.mult)), reads=[t_sA, t_pw_], writes=[t_sA])
                P.add("dve", (lambda e, sA=sA, sl=sl: e.tensor_tensor(out=XT[:, n, sl], in0=XT[:, n, sl], in1=sA[:], op=ALU.add)), reads=[t_sA, t_xt[n]], writes=[t_xt[n]], wr_nosame=True)

        for n in range(8):
            ple_n(n)

        yst = Slots(P, sb, "yst", 2, None, F32, tiles=[HIDr[:, i * 2048:(i + 1) * 2048].bitcast(F32) for i in range(2)])
        for tt in range(17):
            rows = 128 if tt < 16 else NS
            y_, ytok, ych = yst.next()
            for half in range(2):
                pt, t_pt = banks[(tt * 2 + half) % 4]
                for k4 in range(4):
                    kc = half * 4 + k4
                    P.add("pe", (lambda e, pt=pt, k4=k4, kc=kc, tt=tt, rows=rows: e.transpose(out=pt[0:rows, k4 * 128:(k4 + 1) * 128], in_=XT[:, kc, tt * 128:tt * 128 + rows], identity=ident)),
                          reads=[t_xt[kc], t_cst], writes=[t_pt])
                P.add("act", (lambda e, pt=pt, half=half, y_=y_, rows=rows: e.copy(out=y_[0:rows, half * 512:(half + 1) * 512], in_=pt[0:rows, :])),
                      reads=[t_pt], writes=[ytok] + t_hid, wr_nosame=True)
            dst = o_y[tt * 128:(tt + 1) * 128, :] if tt < 16 else o_ys
            outs.append(P.add("sp", (lambda e, y_=y_, dst=dst, rows=rows: e.dma_start(out=dst, in_=y_[0:rows, :])), reads=[ytok], chan=ych))
        P.run(final_dma_ops=outs)
    return nc


_NC_CACHE = {}


def _consts():
    return np.concatenate([np.eye(128), np.triu(np.ones((128, 128))), np.tril(np.ones((128, 128)), -1)], 1).astype(np.float32)


def _gT(g):
    return np.ascontiguousarray(np.asarray(g, np.float32).reshape(8, 128).T)


def kernel(x_prompt, x_sample, cache_k, cache_v, cache_logf, page_table, p_prompt, p_sample,
           ffn1_norm, ffn1_w_gu, ffn1_w_down, mix_norm, w_in, b_forget, q_norm, k_norm,
           gmlp_v_norm, w_spatial, b_spatial, w_proj_attn, w_proj_gmlp, w_out,
           ffn2_norm, ffn2_w_gu, ffn2_w_down, ple_norm, ple_w_gate, ple_w_proj):
    f32 = lambda a: np.ascontiguousarray(np.asarray(a, np.float32))
    x_prompt = f32(x_prompt); x_sample = f32(x_sample)
    cache_k = np.asarray(cache_k); cache_v = np.asarray(cache_v); cache_logf = np.asarray(cache_logf)
    page_table = np.asarray(page_table).astype(np.int32)
    wi = f32(w_in)[0]; bfg = f32(b_forget)[0]; qn = f32(q_norm)[0]; kn = f32(k_norm)[0]; gvn = f32(gmlp_v_norm)[0]
    wsp = f32(w_spatial)[0]; bsp = f32(b_spatial)[0]
    n_pool = cache_k.shape[1]
    consts = _consts()
    w_ina = np.ascontiguousarray(np.concatenate([wi[:, 0:1536], wi[:, 2056:2568], wi[:, 1544:2056], wi[:, 1536:1544]], 1))
    wsT = np.ascontiguousarray(wsp.transpose(2, 0, 1)).reshape(128, 1024)
    bsr = np.zeros((128, 4, 128), np.float32)
    for fc in range(4):
        for gl in range(2):
            bsr[gl * 64:(gl + 1) * 64, fc, :] = bsp[2 * fc + gl][None, :]
    w_gu1 = f32(ffn1_w_gu)[0]; w_dn1 = f32(ffn1_w_down)[0]
    ptT = np.ascontiguousarray(page_table.T)
    xs = np.ascontiguousarray(x_sample[:, 0, :])

    in1 = []
    for c in range(8):
        b, half = c // 2, c % 2
        sm = np.zeros((128, 2048), np.float32)
        sm[:, 0:8] = _gT(f32(ffn1_norm)[0]); sm[:, 8:16] = _gT(f32(mix_norm)[0])
        sm[:, 16:24] = bfg[None, :]
        sm[0:64, 24] = qn; sm[0:64, 25] = kn
        sm[:, 26] = bfg[c]
        for fc in range(4):
            for gl in range(2):
                sm[gl * 64:(gl + 1) * 64, 28 + fc] = wsp[2 * fc + gl, 0, 0]
                sm[gl * 64:(gl + 1) * 64, 32 + fc] = bsp[2 * fc + gl, 0]
        sm[:, 64:576] = np.tile(qn, 8)[None, :]
        sm[:, 576:1088] = np.tile(kn, 8)[None, :]
        sm[:, 1088:1600] = np.tile(gvn, 8)[None, :]
        w_own = np.zeros((1024, 256), np.float32)
        w_own[:, 0:64] = wi[:, c * 64:(c + 1) * 64]; w_own[:, 64:128] = wi[:, 512 + c * 64:512 + (c + 1) * 64]
        w_own[:, 128:192] = wi[:, 1024 + c * 64:1024 + (c + 1) * 64]; w_own[:, 192] = wi[:, 1536 + c]
        in1.append(dict(
            x=np.ascontiguousarray(x_prompt[b, half * NT:(half + 1) * NT]), xs=xs,
            g1T=_gT(f32(ffn1_norm)[0]), gmT=_gT(f32(mix_norm)[0]), w_gu=w_gu1, w_down=w_dn1, w_ina=w_ina, w_own=w_own,
            smalls=sm, consts=consts, wsT=wsT, bsr_in=bsr.reshape(128, 512), ptT=ptT, eye32_in=np.eye(32, dtype=np.float32).reshape(1, 1024),
            cache_k=np.ascontiguousarray(cache_k[0, :, :, c, :], dtype=np.float32).reshape(n_pool, 8192),
            cache_v=np.ascontiguousarray(cache_v[0, :, :, c, :], dtype=np.float32).reshape(n_pool, 8192),
            cache_lf=np.ascontiguousarray(cache_logf[0, :, :, c], dtype=np.float32)))
    key1 = ("l1", n_pool)
    if key1 not in _NC_CACHE:
        _NC_CACHE[key1] = build_l1(n_pool=n_pool)
    r1 = run_bass_kernel_spmd(_NC_CACHE[key1], in1, core_ids=list(range(8))).results
    del in1

    a_all = np.ascontiguousarray(np.concatenate([np.asarray(r1[h]["o_as"], np.float32) for h in range(8)], 0))
    w_ing = np.ascontiguousarray(wi[:, 2568:4616])
    w_gu2 = f32(ffn2_w_gu)[0]; w_dn2 = f32(ffn2_w_down)[0]
    w_pa = f32(w_proj_attn)[0]; w_pg = f32(w_proj_gmlp)[0]; w_o = f32(w_out)[0]; w_plg = f32(ple_w_gate)[0]; w_plp = f32(ple_w_proj)[0]
    p_prompt = f32(p_prompt); pins = np.ascontiguousarray(f32(p_sample)[0, :, 0, :])
    bf = ml_dtypes.bfloat16
    in2 = []
    for c in range(8):
        b, half = c // 2, c % 2
        sm = np.zeros((128, 64), np.float32)
        sm[:, 0:8] = _gT(f32(ffn2_norm)[0]); sm[:, 8:16] = _gT(f32(ple_norm)[0])
        sm[:, 16] = 0.0 if half == 1 else NEG
        own = r1[c]
        if half == 1:
            pr = r1[c - 1]
            kt_prior = np.asarray(pr["o_kt"]); vb_prior = np.asarray(pr["o_vb"]); f_prior = np.asarray(pr["o_f"], np.float32)
        else:
            kt_prior = np.zeros((512, NT), bf); vb_prior = np.zeros((NT, 512), bf); f_prior = np.zeros((NT, 8), np.float32)
        in2.append(dict(
            i_xt=np.asarray(own["o_xt"], np.float32), i_ht=np.asarray(own["o_ht"]), i_qt=np.asarray(own["o_qt"]),
            i_kt=np.ascontiguousarray(np.concatenate([kt_prior, np.asarray(own["o_kt"])], 1)),
            i_vb=np.ascontiguousarray(np.concatenate([vb_prior, np.asarray(own["o_vb"])], 0)),
            i_f=np.ascontiguousarray(np.concatenate([f_prior, np.asarray(own["o_f"], np.float32)], 0)),
            i_mt=np.asarray(own["o_mt"]), i_as=a_all,
            w_ing=w_ing, w_pa=w_pa, w_pg=w_pg, w_o=w_o, w_gu=w_gu2, w_down=w_dn2, w_plg=w_plg, w_plp=w_plp,
            pin=np.ascontiguousarray(p_prompt[0, b, half * NT:(half + 1) * NT]), pins=pins, smalls=sm, consts=consts))
    if "l2" not in _NC_CACHE:
        _NC_CACHE["l2"] = build_l2()
    r2 = run_bass_kernel_spmd(_NC_CACHE["l2"], in2, core_ids=list(range(8))).results

    B, S = 4, 4096
    y_prompt = np.zeros((B, S, D_MODEL), np.float32)
    k_p = np.zeros((1, B, S, 8, 64), np.float32); v_p = np.zeros((1, B, S, 8, 64), np.float32); lf_p = np.zeros((1, B, S, 8), np.float32)
    for c in range(8):
        b, half = c // 2, c % 2
        sl = slice(half * NT, (half + 1) * NT)
        y_prompt[b, sl] = np.asarray(r2[c]["o_y"], np.float32)
        k_p[0, b, sl] = np.asarray(r1[c]["o_k"], np.float32).reshape(NT, 8, 64)
        v_p[0, b, sl] = np.asarray(r1[c]["o_v"], np.float32).reshape(NT, 8, 64)
        lf_p[0, b, sl] = np.asarray(r1[c]["o_lf"], np.float32)
    y_sample = np.asarray(r2[0]["o_ys"], np.float32).reshape(NS, 1, D_MODEL)
    k_s = np.asarray(r1[0]["o_ks"], np.float32).reshape(1, NS, 1, 8, 64)
    v_s = np.asarray(r1[0]["o_vs"], np.float32).reshape(1, NS, 1, 8, 64)
    lf_s = np.asarray(r1[0]["o_lfs"], np.float32).reshape(1, NS, 1, 8)
    gv_s = np.asarray(r1[0]["o_gvs"], np.float32).reshape(1, NS, 1, 8, 64)
    return (y_prompt, y_sample, k_p, v_p, lf_p, k_s, v_s, lf_s, gv_s)
```
